# Optimizing a Trainium2 kernel written in Bass

```python
import jax, jax.numpy as jnp
from jax import lax
import numpy as np

D_MODEL = 2048
BATCH = 4
SEQ = 4096
DEPTH = 4

HEAD_DIM = 128
ATTN_WIDTH = D_MODEL // 2
N_Q_HEADS = ATTN_WIDTH // HEAD_DIM
N_KV_HEADS = N_Q_HEADS // 4
KV_WIDTH = N_KV_HEADS * HEAD_DIM
CONV_WIDTH = D_MODEL // 4
CONV_KERNEL = 31
SGU_WIDTH = D_MODEL // 4
SGU_HEAD_DIM = 128
SGU_HEADS = SGU_WIDTH // SGU_HEAD_DIM
CHUNK = 128
MIX_WIDTH = ATTN_WIDTH + CONV_WIDTH + SGU_WIDTH
IN_WIDTH = ATTN_WIDTH + 2 * KV_WIDTH + 2 * CONV_WIDTH + 2 * SGU_WIDTH
WINDOW = 128
BLOCK = 128
ROPE_THETA = 500000.0
ROT_DIM = HEAD_DIM // 4
D_FF = ((8 * D_MODEL // 3 + 255) // 256) * 256
EPS = 1e-6

kernel_name = "hybrid_parallel_groups_encoder"


def rms_norm(x, g):
    xf = x.astype(jnp.float32)
    y = xf * lax.rsqrt(jnp.mean(xf * xf, axis=-1, keepdims=True) + EPS)
    return (y * g.astype(jnp.float32)).astype(x.dtype)


def layer_norm(x, g, b):
    xf = x.astype(jnp.float32)
    mu = jnp.mean(xf, axis=-1, keepdims=True)
    var = jnp.mean(jnp.square(xf - mu), axis=-1, keepdims=True)
    y = (xf - mu) * lax.rsqrt(var + EPS)
    return (y * g.astype(jnp.float32) + b.astype(jnp.float32)).astype(x.dtype)


def rope_tables(seq):
    pos = jnp.arange(seq, dtype=jnp.float32)
    inv = ROPE_THETA ** (-jnp.arange(0, ROT_DIM, 2, dtype=jnp.float32) / ROT_DIM)
    ang = pos[:, None] * inv[None, :]
    return jnp.cos(ang), jnp.sin(ang)


def partial_rope(t, cos, sin):
    half = ROT_DIM // 2
    t1 = t[..., :half].astype(jnp.float32)
    t2 = t[..., half:ROT_DIM].astype(jnp.float32)
    c = cos[None, :, None, :]
    s = sin[None, :, None, :]
    rot = jnp.concatenate([t1 * c - t2 * s, t2 * c + t1 * s], axis=-1).astype(t.dtype)
    return jnp.concatenate([rot, t[..., ROT_DIM:]], axis=-1)


def windowed_gqa_sink(q, k, v, sink):
    B, S, H, Dh = q.shape
    G = k.shape[2]
    R = H // G
    nb = S // BLOCK
    pad = ((0, 0), (BLOCK, BLOCK), (0, 0), (0, 0))
    kp = jnp.pad(k, pad).reshape(B, nb + 2, BLOCK, G, Dh)
    vp = jnp.pad(v, pad).reshape(B, nb + 2, BLOCK, G, Dh)
    kw = jnp.concatenate([kp[:, :-2], kp[:, 1:-1], kp[:, 2:]], axis=2)
    vw = jnp.concatenate([vp[:, :-2], vp[:, 1:-1], vp[:, 2:]], axis=2)
    qb = q.reshape(B, nb, BLOCK, G, R, Dh)
    scale = 1.0 / float(np.sqrt(Dh))
    s = jnp.einsum('bnqgrd,bnkgd->bngrqk', qb, kw).astype(jnp.float32) * scale
    qpos = jnp.arange(S).reshape(nb, BLOCK)
    kpos = jnp.arange(nb)[:, None] * BLOCK - BLOCK + jnp.arange(3 * BLOCK)[None, :]
    valid = ((kpos[:, None, :] >= 0) & (kpos[:, None, :] < S)
             & (jnp.abs(qpos[:, :, None] - kpos[:, None, :]) <= WINDOW))
    s = jnp.where(valid[None, :, None, None], s, jnp.finfo(jnp.float32).min)
    sk = jnp.broadcast_to(sink.astype(jnp.float32).reshape(1, 1, G, R, 1, 1), s.shape[:-1] + (1,))
    p = jax.nn.softmax(jnp.concatenate([s, sk], axis=-1), axis=-1)[..., :-1]
    o = jnp.einsum('bngrqk,bnkgd->bnqgrd', p.astype(v.dtype), vw)
    return o.reshape(B, S, H * Dh)


def conformer_conv(a, gate, dw_w, dw_b, ln_g, ln_b):
    c = a * jax.nn.sigmoid(gate)
    C = c.shape[-1]
    c = lax.conv_general_dilated(
        c, dw_w[:, None, :].astype(c.dtype), window_strides=(1,),
        padding=[((CONV_KERNEL - 1) // 2, (CONV_KERNEL - 1) // 2)],
        dimension_numbers=('NWC', 'WIO', 'NWC'), feature_group_count=C) + dw_b
    c = layer_norm(c, ln_g, ln_b)
    return jax.nn.silu(c)


def spatial_gating(uv, ln_g, ln_b, w_s, b_s):
    B, S, _ = uv.shape
    uv = jax.nn.gelu(uv, approximate=False)
    u, v = jnp.split(uv, 2, axis=-1)
    v = layer_norm(v, ln_g, ln_b)
    vc = v.reshape(B, S // CHUNK, CHUNK, SGU_HEADS, SGU_HEAD_DIM)
    sp = jnp.einsum('hpq,bcqhe->bcphe', w_s, vc) + b_s.T[None, None, :, :, None]
    return u * sp.reshape(B, S, SGU_WIDTH)


def hybrid_mixer(h, w_in, sink, dw_w, dw_b, cln_g, cln_b, sln_g, sln_b, sgu_w, sgu_b, w_out, cos, sin):
    B, S, _ = h.shape
    z = h @ w_in
    offs = np.cumsum([ATTN_WIDTH, KV_WIDTH, KV_WIDTH, CONV_WIDTH, CONV_WIDTH]).tolist()
    q, k, v, ca, cg, uv = jnp.split(z, offs, axis=-1)
    q = partial_rope(q.reshape(B, S, N_Q_HEADS, HEAD_DIM), cos, sin)
    k = partial_rope(k.reshape(B, S, N_KV_HEADS, HEAD_DIM), cos, sin)
    v = v.reshape(B, S, N_KV_HEADS, HEAD_DIM)
    attn = windowed_gqa_sink(q, k, v, sink)
    conv = conformer_conv(ca, cg, dw_w, dw_b, cln_g, cln_b)
    sgu = spatial_gating(uv, sln_g, sln_b, sgu_w, sgu_b)
    return jnp.concatenate([attn.astype(h.dtype), conv.astype(h.dtype), sgu.astype(h.dtype)], axis=-1) @ w_out


def swiglu(h, w_gate, w_up, w_down):
    return (jax.nn.silu(h @ w_gate) * (h @ w_up)) @ w_down


def setup_inputs(seed: int = 0) -> dict:
    key = jax.random.key(seed)
    ks = jax.random.split(key, 20)
    f32 = jnp.float32
    nrm = lambda k, shape, sc: jax.random.normal(k, shape, f32) * sc
    return {
        "x": nrm(ks[0], (BATCH, SEQ, D_MODEL), 1.0),
        "mix_norm_g": 1.0 + nrm(ks[1], (DEPTH, D_MODEL), 0.02),
        "w_in": nrm(ks[2], (DEPTH, D_MODEL, IN_WIDTH), D_MODEL ** -0.5),
        "sink": nrm(ks[3], (DEPTH, N_Q_HEADS), 0.5),
        "conv_dw_w": nrm(ks[4], (DEPTH, CONV_KERNEL, CONV_WIDTH), CONV_KERNEL ** -0.5),
        "conv_dw_b": nrm(ks[5], (DEPTH, CONV_WIDTH), 0.02),
        "conv_ln_g": 1.0 + nrm(ks[6], (DEPTH, CONV_WIDTH), 0.02),
        "conv_ln_b": nrm(ks[7], (DEPTH, CONV_WIDTH), 0.02),
        "sgu_ln_g": 1.0 + nrm(ks[8], (DEPTH, SGU_WIDTH), 0.02),
        "sgu_ln_b": nrm(ks[9], (DEPTH, SGU_WIDTH), 0.02),
        "sgu_w": nrm(ks[10], (DEPTH, SGU_HEADS, CHUNK, CHUNK), CHUNK ** -0.5),
        "sgu_b": 1.0 + nrm(ks[11], (DEPTH, SGU_HEADS, CHUNK), 0.02),
        "w_out": nrm(ks[12], (DEPTH, MIX_WIDTH, D_MODEL), MIX_WIDTH ** -0.5),
        "ffn_norm_g": 1.0 + nrm(ks[13], (DEPTH, D_MODEL), 0.02),
        "w_gate": nrm(ks[14], (DEPTH, D_MODEL, D_FF), D_MODEL ** -0.5),
        "w_up": nrm(ks[15], (DEPTH, D_MODEL, D_FF), D_MODEL ** -0.5),
        "w_down": nrm(ks[16], (DEPTH, D_FF, D_MODEL), D_FF ** -0.5),
        "final_norm_g": 1.0 + nrm(ks[17], (D_MODEL,), 0.02),
    }


def reference(x, mix_norm_g, w_in, sink, conv_dw_w, conv_dw_b, conv_ln_g, conv_ln_b,
              sgu_ln_g, sgu_ln_b, sgu_w, sgu_b, w_out, ffn_norm_g, w_gate, w_up, w_down,
              final_norm_g):
    cos, sin = rope_tables(x.shape[1])
    for l in range(DEPTH):
        h = rms_norm(x, mix_norm_g[l])
        x = x + hybrid_mixer(h, w_in[l], sink[l], conv_dw_w[l], conv_dw_b[l], conv_ln_g[l],
                             conv_ln_b[l], sgu_ln_g[l], sgu_ln_b[l], sgu_w[l], sgu_b[l],
                             w_out[l], cos, sin)
        h = rms_norm(x, ffn_norm_g[l])
        x = x + swiglu(h, w_gate[l], w_up[l], w_down[l])
    return rms_norm(x, final_norm_g)
```

```python
import contextlib
import os
import types
from math import prod

import numpy as np
import concourse.bass as bass
import concourse.mybir as mybir
from concourse.bass_utils import run_bass_kernel_spmd

F32 = mybir.dt.float32
BF16 = mybir.dt.bfloat16
AF = mybir.ActivationFunctionType
ALU = mybir.AluOpType

D = 2048
SEQ = 4096
BATCH = 4
DEPTH = 4
T = 2560
OWN = 2048
NT = T // 128
NB = T // 512
KC = D // 128
DFF = 5632
NF = DFF // 128
NPASS = 4
FPP = NF // NPASS
NWIN = 28
EPS = 1e-6
NEG = -30000.0

ENGS = ["pe", "act", "dve", "pool", "sp"]
SAME_ENGINE_SYNC = True


def _freeze(fn):
    if fn.__closure__ is None:
        return fn
    cells = []
    for c in fn.__closure__:
        try:
            cells.append(types.CellType(c.cell_contents))
        except ValueError:
            cells.append(c)
    g = types.FunctionType(fn.__code__, fn.__globals__, fn.__name__, fn.__defaults__, tuple(cells))
    g.__kwdefaults__ = fn.__kwdefaults__
    return g


class Res:
    __slots__ = ("w", "r")

    def __init__(self):
        self.w = None
        self.r = []


class DSem:
    __slots__ = ("key", "count")

    def __init__(self, key):
        self.key = key
        self.count = 0


class Builder:
    def __init__(self, nc, stack):
        self.nc = nc
        self.stack = stack
        self.streams = {e: [] for e in ENGS}
        self.sems = []
        self.cnt = {}
        self.waited = {e: {} for e in ENGS}
        self.esem = {}
        for e in ENGS:
            self.esem[e] = self.new_sem("e_" + e)
            self.cnt[e] = 0
        self.n_ins = 0

    def new_sem(self, name):
        h = self.stack.enter_context(self.nc.semaphore(name))
        self.sems.append(h)
        return len(self.sems) - 1

    def dsem_pool(self, name, n):
        ds = [DSem(self.new_sem(f"{name}{i}")) for i in range(n)]
        self.all_ds = getattr(self, "all_ds", []) + ds
        return ds

    def sb(self, name, shape, dtype):
        return self.stack.enter_context(self.nc.sbuf_tensor(name, list(shape), dtype))[:]

    def ps(self, name, shape, dtype):
        return self.stack.enter_context(self.nc.psum_tensor(name, list(shape), dtype))[:]

    def _wait(self, eng, ev):
        if ev is None:
            return
        s, v = ev
        if s == self.esem[eng] and (eng == "pe" or not SAME_ENGINE_SYNC):
            return
        if self.waited[eng].get(s, 0) >= v:
            return
        self.waited[eng][s] = v
        self.streams[eng].append(("wait", s, v))

    def _deps(self, eng, reads, writes):
        for r in reads:
            self._wait(eng, r.w)
        for w in writes:
            self._wait(eng, w.w)
            for ev in w.r:
                self._wait(eng, ev)

    def _mark(self, ev, reads, writes):
        for r in reads:
            r.r = [e for e in r.r if e[0] != ev[0]]
            r.r.append(ev)
        for w in writes:
            w.w = ev
            w.r = []

    def op(self, eng, fn, reads=(), writes=()):
        self._deps(eng, reads, writes)
        self.cnt[eng] += 1
        ev = (self.esem[eng], self.cnt[eng])
        self.streams[eng].append(("ins", _freeze(fn), ev[0], 1))
        self._mark(ev, reads, writes)
        self.n_ins += 1
        return ev

    def group(self, eng, fns, reads=(), writes=()):
        self._deps(eng, reads, writes)
        for fn in fns[:-1]:
            self.streams[eng].append(("ins", _freeze(fn), None, 0))
        self.cnt[eng] += 1
        ev = (self.esem[eng], self.cnt[eng])
        self.streams[eng].append(("ins", _freeze(fns[-1]), ev[0], 1))
        self._mark(ev, reads, writes)
        self.n_ins += len(fns)
        return ev

    def dma(self, q, dsem, out, in_, reads=(), writes=()):
        self._deps(q, reads, writes)
        if dsem.count:
            self._wait(q, (dsem.key, dsem.count))
        dsem.count += 16
        ev = (dsem.key, dsem.count)
        self.streams[q].append(("ins", lambda e, o=out, i=in_: e.dma_start(out=o, in_=i), ev[0], 16))
        self._mark(ev, reads, writes)
        self.n_ins += 1
        return ev

    def replay(self):
        nc = self.nc
        sems = self.sems
        streams = self.streams

        def run(e, st):
            for it in st:
                if it[0] == "wait":
                    e.wait_ge(sems[it[1]], it[2])
                else:
                    ins = it[1](e)
                    if it[2] is not None:
                        ins.then_inc(sems[it[2]], it[3])

        with nc.Block() as block:
            @block.tensor
            def _(e):
                run(e, streams["pe"])

            @block.scalar
            def _(e):
                run(e, streams["act"])

            @block.vector
            def _(e):
                run(e, streams["dve"])

            @block.gpsimd
            def _(e):
                run(e, streams["pool"])

            @block.sync
            def _(e):
                run(e, streams["sp"])


class Ring:
    def __init__(self, b, name, views, ndsem=0, mk=None):
        self.views = views
        self.res = [(mk.newB() if mk is not None else Res()) for _ in views]
        if ndsem and not isinstance(ndsem, int):
            self.ds = ndsem
            assert len(self.ds) >= len(views)
        else:
            self.ds = b.dsem_pool(name, len(views)) if ndsem else None
        self.i = 0

    def next(self):
        k = self.i % len(self.views)
        self.i += 1
        if self.ds:
            return self.views[k], self.res[k], self.ds[k]
        return self.views[k], self.res[k]


def _const_layout(L):
    off = {}
    o = 0
    for name, n in [("g1", L * 16), ("g2", L * 16), ("gf", 16), ("dww", L * 4 * 31),
                    ("dwb", L * 4), ("clg", L * 4), ("clb", L * 4), ("sink", L * 8),
                    ("ident", 128), ("rm", 32)]:
        off[name] = (o, n)
        o += n
    return off, o


PL_WORDS = 512 + 512 + 512 + 512


class MK:
    def __init__(self, L, debug=False, l0=0):
        self.L = L
        self.debug = debug
        nc = self.nc = bass.Bass("TRN2", target_bir_lowering=False)
        dk = "ExternalOutput" if debug else None

        def dram(name, shape, dt, kind=None):
            if kind is None:
                return nc.dram_tensor(name, list(shape), dt).ap()
            return nc.dram_tensor(name, list(shape), dt, kind=kind).ap()

        self.coff, self.cwords = _const_layout(L)
        self.x_in = dram("x_in", [T, D], F32, "ExternalInput")
        self.w_in = dram("w_in", [L, NWIN, 128, KC, 128], F32, "ExternalInput")
        self.w_out = dram("w_out", [L, KC, 128, KC, 128], F32, "ExternalInput")
        self.w_gate = dram("w_gate", [L, NF, 128, KC, 128], F32, "ExternalInput")
        self.w_up = dram("w_up", [L, NF, 128, KC, 128], F32, "ExternalInput")
        self.w_down = dram("w_down", [L, NPASS, KC, 128, FPP, 128], F32, "ExternalInput")
        self.cst = dram("cst", [128, self.cwords], F32, "ExternalInput")
        self.plc = dram("plc", [L, 128, PL_WORDS], F32, "ExternalInput")
        self.rope = dram("rope", [32, 2, T], F32, "ExternalInput")
        self.msk = dram("msk", [128, 2, 512], F32, "ExternalInput")
        self.out = dram("out", [T, D], F32, "ExternalOutput")
        self.xT = dram("xT", [T // 256, 128, KC, 256], F32, dk)
        self.qT = dram("qT", [8, 128, T], BF16, dk)
        self.kT = dram("kT", [2, 128, T], BF16, dk)
        self.V = dram("V", [128, NT, 256], BF16, dk)
        self.cT = dram("cT", [4, 128, T], F32, dk)
        self.uT = dram("uT", [4, 128, T], BF16, dk)
        self.vg = dram("vg", [128, NT, 512], F32, dk)
        if debug:
            self.dbgA = dram("dbgA", [128, KC, T], BF16, dk)

    def setup(self, st):
        nc = self.nc
        b = self.b = Builder(nc, st)
        L = self.L
        A_W = KC * T // 2
        B_W = 14080
        WS_W = 2 * 2048
        WB_W = 3 * 1024
        self.A = b.sb("A", [128, A_W], F32)
        self.Bm = b.sb("Bm", [128, B_W], F32)
        self.WS = b.sb("WS", [128, WS_W], F32)
        self.WB = b.sb("WB", [128, WB_W], F32)
        self.CS = b.sb("CS", [128, self.cwords], F32)
        self.PL = b.sb("PL", [128, PL_WORDS], F32)
        self.wsT_bf = b.sb("wsT_bf", [128, 4, 128], BF16)
        self.mask_bf = b.sb("mask_bf", [128, 2, 512], BF16)
        self.ones_bf = b.sb("ones_bf", [128, 128], BF16)
        self.rm_bf = b.sb("rm_bf", [128, 128], BF16)
        self.ident_bf = b.sb("ident_bf", [128, 128], BF16)
        self.esink = b.sb("esink", [128, 8], F32)
        self.A_bf = self.A[:, :].bitcast(BF16).rearrange("p (a b) -> p a b", a=KC)
        self.Ares = [Res() for _ in range(NB)]
        self.tmpf = Ring(b, "tmpf", [b.sb(f"tmpf{i}", [128, 512], F32) for i in range(2)])
        self.obf = Ring(b, "obf", [b.sb(f"obf{i}", [128, 512], BF16) for i in range(3)], ndsem=1)
        self.xin = Ring(b, "xin", [b.sb(f"xin{i}", [128, 512], F32) for i in range(2)], ndsem=1)
        self.xout = Ring(b, "xout", [b.sb(f"xout{i}", [128, 512], F32) for i in range(2)], ndsem=1)
        self.of32 = self.xout
        self.small = Ring(b, "small", [b.sb(f"small{i}", [128, 512], F32) for i in range(2)])
        self.banks = [b.ps(f"bank{i}", [128, 512], F32) for i in range(8)]
        self.bres = [Res() for _ in range(8)]
        self.bank_i = 0
        ws3 = self.WS[:, :].rearrange("p (s k c) -> p s k c", s=2, k=KC)
        wb3 = self.WB[:, :].bitcast(BF16).rearrange("p (s k c) -> p s k c", s=3, k=KC)
        self.wstage = [(ws3[:, i], Res(), d) for i, d in enumerate(b.dsem_pool("wld", 2))]
        self.wbf = [(wb3[:, i], Res()) for i in range(3)]
        self.misc_ds = b.dsem_pool("misc", 4)
        self.misc_i = 0
        self.xres = [[Res() for _ in range(T // 256)] for _ in range(KC)]
        self.dres = {}
        self.cs_res = Res()
        self.pl_res = Res()
        self.b_fence = []
        self.b_res = []
        self.cv_ds = b.dsem_pool("cvl", 4)
        self.ds_p = b.dsem_pool("dsp", 2)
        self.ds_q = b.dsem_pool("dsq", 2)

    def newB(self):
        r = Res()
        r.r = list(self.b_fence)
        self.b_res.append(r)
        return r

    def b_phase(self):
        best = {}
        for ev in self.b_fence:
            best[ev[0]] = max(best.get(ev[0], 0), ev[1])
        for r in self.b_res:
            for ev in ([r.w] if r.w else []) + r.r:
                best[ev[0]] = max(best.get(ev[0], 0), ev[1])
        self.b_fence = [(k, v) for k, v in best.items()]
        self.b_res = []

    def dr(self, *key):
        r = self.dres.get(key)
        if r is None:
            r = self.dres[key] = Res()
        return r

    def mds(self):
        d = self.misc_ds[self.misc_i % len(self.misc_ds)]
        self.misc_i += 1
        return d

    def bank(self, lo=0, hi=4):
        n = hi - lo
        k = lo + (self.bank_i % n)
        self.bank_i += 1
        return self.banks[k], self.bres[k]

    def cview(self, name):
        o, n = self.coff[name]
        return self.CS[:, o:o + n]

    def bview(self, off, shape, dtype):
        n = prod(shape[1:])
        nw = n // 2 if dtype == BF16 else n
        assert off + nw <= self.Bm.shape[1], (off, nw)
        ap = self.Bm[:, off:off + nw]
        if dtype == BF16:
            ap = ap.bitcast(BF16)
        if len(shape) == 3:
            ap = ap.rearrange("p (a b) -> p a b", a=shape[1])
        return ap

    def wstream_init(self, tiles):
        self.wtiles = tiles
        self.w_cast = 0
        self.w_used = 0
        self.w_loaded = 0
        for _ in range(2):
            self._wload()

    def _wload(self):
        n = self.w_loaded
        if n >= len(self.wtiles):
            return
        ap, kc = self.wtiles[n]
        sv, sr, ds = self.wstage[n % 2]
        self.b.dma("sp", ds, sv[:, 0:kc, :], ap, writes=[sr])
        self.w_loaded += 1

    def _wcast(self):
        n = self.w_cast
        if n >= len(self.wtiles):
            return
        _, kc = self.wtiles[n]
        sv, sr, _ = self.wstage[n % 2]
        bv, br = self.wbf[n % 3]
        self.b.op("pool", lambda e, o=bv[:, 0:kc, :], i=sv[:, 0:kc, :]: e.tensor_copy(out=o, in_=i),
                  reads=[sr], writes=[br])
        self.w_cast += 1
        self._wload()

    def wnext(self, ahead=1):
        n = self.w_used
        while self.w_cast <= min(n + ahead, len(self.wtiles) - 1):
            self._wcast()
        self.w_used += 1
        return self.wbf[n % 3]

    def rsqrt_eps(self, ap, res):
        b = self.b
        b.op("dve", lambda e: e.tensor_scalar(out=ap, in0=ap, scalar1=EPS, scalar2=None, op0=ALU.add),
             reads=[res], writes=[res])
        b.op("act", lambda e: e.activation(out=ap, in_=ap, func=AF.Sqrt), reads=[res], writes=[res])
        b.op("dve", lambda e: e.reciprocal(out=ap, in_=ap), reads=[res], writes=[res])

    def gemm_fm(self, wt, wr, kc_n, act, ares, blk, bank, bres, n=512):
        fns = []
        for kc in range(kc_n):
            fns.append(lambda e, kc=kc: e.matmul(bank[:, 0:n], lhsT=wt[:, kc, :],
                                                 rhs=act[:, kc, blk * 512:blk * 512 + n],
                                                 start=(kc == 0), stop=(kc == kc_n - 1)))
        self.b.group("pe", fns, reads=[wr, ares], writes=[bres])

    def load_consts(self):
        b = self.b
        b.dma("sp", self.mds(), self.CS[:, :], self.cst, writes=[self.cs_res])
        r = Res()
        b.op("pool", lambda e: e.memset(self.ones_bf[:, :], 1.0), writes=[r])
        b.op("pool", lambda e: e.memset(self.rm_bf[:, :], 0.0), writes=[r])
        b.op("pool", lambda e: e.tensor_copy(out=self.rm_bf[:, 0:32], in_=self.cview("rm")),
             reads=[self.cs_res], writes=[r])
        b.op("pool", lambda e: e.tensor_copy(out=self.ident_bf[:, :], in_=self.cview("ident")),
             reads=[self.cs_res], writes=[r])
        mtmp = self.bview(0, [128, 2, 512], F32)
        mr = self.newB()
        b.dma("sp", self.mds(), mtmp, self.msk, writes=[mr])
        b.op("pool", lambda e: e.tensor_copy(out=self.mask_bf[:, :, :], in_=mtmp), reads=[mr], writes=[r])
        self.k_res = r

    def phase_in(self):
        b = self.b
        xt_views = [self.bview(i * 2048, [128, 2048], F32) for i in range(2)]
        xo_views = [self.bview(4096 + i * 4096, [128, KC, 256], F32) for i in range(2)]
        self.b_phase()
        xt_ring = Ring(b, "pin_l", xt_views, ndsem=self.ds_p, mk=self)
        xo_ring = Ring(b, "pin_s", xo_views, ndsem=self.ds_q, mk=self)
        ident = self.cview("ident")
        for hb in range(T // 256):
            xo, xor_, xod = xo_ring.next()
            for tt in range(2):
                t = 2 * hb + tt
                xt, xr, xd = xt_ring.next()
                b.dma("sp", xd, xt, self.x_in[t * 128:(t + 1) * 128, :], writes=[xr])
                for g in range(4):
                    bk, br = self.bank()
                    fns = [lambda e, j=j: e.transpose(out=bk[:, j * 128:(j + 1) * 128],
                                                      in_=xt[:, (4 * g + j) * 128:(4 * g + j + 1) * 128],
                                                      identity=ident) for j in range(4)]
                    b.group("pe", fns, reads=[xr, self.cs_res], writes=[br])
                    src = bk[:, :].rearrange("p (a b) -> p a b", a=4)
                    dst = xo[:, 4 * g:4 * g + 4, tt * 128:(tt + 1) * 128]
                    if g % 2 == 0:
                        b.op("dve", lambda e: e.tensor_copy(out=dst, in_=src), reads=[br], writes=[xor_])
                    else:
                        b.op("act", lambda e: e.activation(out=dst, in_=src, func=AF.Copy), reads=[br], writes=[xor_])
            b.dma("sp", xod, self.xT[hb], xo, reads=[xor_], writes=[self.xres[kc][hb] for kc in range(KC)])

    def phase_norm(self, gname, l):
        b = self.b
        go, _ = self.coff[gname]
        gv = self.CS[:, go + l * 16: go + (l + 1) * 16]
        self.b_phase()
        xs_ring = Ring(b, f"nrm_x{gname}{l}", [self.bview(i * 4096, [128, KC, 256], F32) for i in range(2)], ndsem=self.ds_p, mk=self)
        sq_ring = Ring(b, "nrm_q", [self.bview(8192 + i * 2048, [128, KC, 256], BF16) for i in range(2)], mk=self)
        rs_ring = Ring(b, "nrm_r", [self.bview(12288 + i * 256, [128, 256], F32) for i in range(2)], mk=self)
        for hb in range(T // 256):
            blk = hb // 2
            t0 = hb * 256
            xs, xr, xd = xs_ring.next()
            b.dma("sp", xd, xs, self.xT[hb], reads=[self.xres[kc][hb] for kc in range(KC)], writes=[xr])
            sq, sqr = sq_ring.next()
            b.op("act", lambda e, sq=sq, xs=xs: e.activation(out=sq, in_=xs, func=AF.Square), reads=[xr], writes=[sqr])
            bk, br = self.bank(4, 6)
            fns = [lambda e, kc=kc, bk=bk, sq=sq: e.matmul(bk[:, 0:256], lhsT=self.ones_bf[:, :], rhs=sq[:, kc, :],
                                                          start=(kc == 0), stop=(kc == KC - 1)) for kc in range(KC)]
            b.group("pe", fns, reads=[sqr, self.k_res], writes=[br])
            rs, rsr = rs_ring.next()
            b.op("dve", lambda e, rs=rs, bk=bk: e.tensor_scalar(out=rs, in0=bk[:, 0:256], scalar1=1.0 / D, scalar2=None,
                                                               op0=ALU.mult), reads=[br], writes=[rsr])
            self.rsqrt_eps(rs, rsr)
            for kc in range(KC):
                b.op("dve", lambda e, kc=kc, xs=xs, rs=rs, t0=t0: e.scalar_tensor_tensor(
                    out=self.A_bf[:, kc, t0:t0 + 256], in0=xs[:, kc, :], scalar=gv[:, kc:kc + 1], in1=rs,
                    op0=ALU.mult, op1=ALU.mult), reads=[xr, rsr, self.cs_res], writes=[self.Ares[blk]])

    def load_layer_consts(self, l):
        b = self.b
        b.dma("sp", self.mds(), self.PL[:, :], self.plc[l], writes=[self.pl_res])
        b.op("pool", lambda e: e.tensor_copy(out=self.wsT_bf[:, :, :],
                                             in_=self.PL[:, 1536:2048].rearrange("p (a b) -> p a b", a=4)),
             reads=[self.pl_res], writes=[self.pl_res])
        so, _ = self.coff["sink"]
        b.op("act", lambda e: e.activation(out=self.esink[:, :], in_=self.CS[:, so + l * 8: so + l * 8 + 8], func=AF.Exp),
             reads=[self.cs_res], writes=[self.pl_res])

    def phase_proj(self, l):
        b = self.b
        A = self.A_bf
        ropev = self.bview(0, [128, 2 * T], F32)[0:32, :].rearrange("p (a b) -> p a b", a=2)
        self.b_phase()
        rope_r = self.newB()
        b.dma("sp", self.mds(), ropev, self.rope, writes=[rope_r])
        self.q32 = Ring(b, "q32", [self.bview(5120 + i * 512, [128, 512], F32) for i in range(2)], mk=self)
        self.t1 = Ring(b, "t1", [self.bview(6144 + i * 512, [128, 512], F32)[0:32, :] for i in range(2)], mk=self)
        self.t2 = Ring(b, "t2", [self.bview(7168 + i * 512, [128, 512], F32)[0:32, :] for i in range(2)], mk=self)
        qh_ring = Ring(b, "qh", [self.bview(8192 + i * 256, [128, 512], BF16) for i in range(2)], mk=self)
        ql_ring = Ring(b, "ql", [self.bview(8704 + i * 256, [128, 512], BF16) for i in range(2)], mk=self)
        rmv = self.rm_bf
        for j in range(4):
            wa, war = self.wnext(ahead=2)
            wg, wgr = self.wnext(ahead=1)
            for blk in range(NB):
                pa, par = self.bank()
                self.gemm_fm(wa, war, KC, A, self.Ares[blk], blk, pa, par)
                pg, pgr = self.bank()
                self.gemm_fm(wg, wgr, KC, A, self.Ares[blk], blk, pg, pgr)
                tf, tfr = self.tmpf.next()
                b.op("act", lambda e, tf=tf, pg=pg: e.activation(out=tf, in_=pg[:, :], func=AF.Sigmoid),
                     reads=[pgr], writes=[tfr])
                of, ofr, ofd = self.of32.next()
                b.op("dve", lambda e, of=of, pa=pa, tf=tf: e.tensor_tensor(out=of, in0=pa[:, :], in1=tf, op=ALU.mult),
                     reads=[par, tfr], writes=[ofr])
                b.dma("sp", ofd, self.cT[j, :, blk * 512:(blk + 1) * 512], of, reads=[ofr],
                      writes=[self.dr("cT", j, blk)])
        import os
        pstop = int(os.environ.get("PROJ_STOP", "9"))
        if pstop <= 1:
            return
        for h in range(10):
            wt, wr = self.wnext()
            dst = self.qT[h] if h < 8 else self.kT[h - 8]
            for blk in range(NB):
                bk, br = self.bank()
                self.gemm_fm(wt, wr, KC, A, self.Ares[blk], blk, bk, br)
                ob, obr, obd = self.obf.next()
                rmode = int(os.environ.get("ROPE_MODE", "2"))
                if rmode == 0:
                    b.op("act", lambda e, ob=ob, bk=bk: e.activation(out=ob, in_=bk[:, :], func=AF.Copy),
                         reads=[br], writes=[obr])
                    b.dma("sp", obd, dst[:, blk * 512:(blk + 1) * 512], ob, reads=[obr],
                          writes=[self.dr("qk", h, blk)])
                    continue
                q32, q32r = self.q32.next()
                b.op("dve", lambda e, q32=q32, bk=bk: e.tensor_copy(out=q32, in_=bk[:, :]), reads=[br], writes=[q32r])
                b.op("act", lambda e, ob=ob, q32=q32: e.activation(out=ob, in_=q32, func=AF.Copy),
                     reads=[q32r], writes=[obr])
                pr, prr = self.bank(6, 8)
                qh, qhr = qh_ring.next()
                ql, qlr = ql_ring.next()
                b.op("dve", lambda e: e.tensor_copy(out=qh, in_=q32), reads=[q32r], writes=[qhr])
                b.op("dve", lambda e: e.tensor_tensor(out=ql, in0=q32, in1=qh, op=ALU.subtract),
                     reads=[q32r, qhr], writes=[qlr])
                b.group("pe", [lambda e: e.matmul(pr[:, :], lhsT=rmv[:, :], rhs=qh, start=True, stop=False),
                               lambda e: e.matmul(pr[:, :], lhsT=rmv[:, :], rhs=ql, start=False, stop=True)],
                        reads=[qhr, qlr, self.k_res], writes=[prr])
                if rmode == 1:
                    b.op("dve", lambda e: e.tensor_copy(out=ob[0:32, :], in_=pr[0:32, :]), reads=[prr], writes=[obr])
                    b.dma("sp", obd, dst[:, blk * 512:(blk + 1) * 512], ob, reads=[obr],
                          writes=[self.dr("qk", h, blk)])
                    continue
                t1, t1r = self.t1.next()
                b.op("dve", lambda e, t1=t1, q32=q32, blk=blk: e.tensor_tensor(
                    out=t1, in0=q32[0:32, :], in1=ropev[:, 0, blk * 512:(blk + 1) * 512], op=ALU.mult),
                    reads=[q32r, rope_r], writes=[t1r])
                t2, t2r = self.t2.next()
                b.op("dve", lambda e, t2=t2, pr=pr, blk=blk: e.tensor_tensor(
                    out=t2, in0=pr[0:32, :], in1=ropev[:, 1, blk * 512:(blk + 1) * 512], op=ALU.mult),
                    reads=[prr, rope_r], writes=[t2r])
                b.op("dve", lambda e, ob=ob, t1=t1, t2=t2: e.tensor_tensor(out=ob[0:32, :], in0=t1, in1=t2, op=ALU.add),
                     reads=[t1r, t2r], writes=[obr])
                b.dma("sp", obd, dst[:, blk * 512:(blk + 1) * 512], ob, reads=[obr],
                      writes=[self.dr("qk", h, blk)])
        if pstop <= 2:
            return
        wv0, wv0r = self.wnext(ahead=2)
        wv1, wv1r = self.wnext(ahead=1)
        for i2 in range(NT // 2):
            bk, br = self.bank()
            fns = []
            for jj in range(2):
                t = 2 * i2 + jj
                for c, wt in enumerate((wv0, wv1)):
                    col = jj * 256 + c * 128
                    for kc in range(KC):
                        fns.append(lambda e, t=t, kc=kc, wt=wt, col=col: e.matmul(
                            bk[:, col:col + 128], lhsT=A[:, kc, t * 128:(t + 1) * 128], rhs=wt[:, kc, :],
                            start=(kc == 0), stop=(kc == KC - 1)))
            b.group("pe", fns, reads=[wv0r, wv1r, self.Ares[i2 // 2]], writes=[br])
            ob, obr, obd = self.obf.next()
            b.op("act", lambda e: e.activation(out=ob, in_=bk[:, :], func=AF.Copy), reads=[br], writes=[obr])
            b.dma("sp", obd, self.V[:, 2 * i2:2 * i2 + 2, :], ob.rearrange("p (a b) -> p a b", a=2),
                  reads=[obr], writes=[self.dr("V", i2)])
        if pstop <= 3:
            return
        for j in range(4):
            wt, wr = self.wnext()
            for blk in range(NB):
                bk, br = self.bank()
                self.gemm_fm(wt, wr, KC, A, self.Ares[blk], blk, bk, br)
                ob, obr, obd = self.obf.next()
                b.op("act", lambda e, ob=ob, bk=bk: e.activation(out=ob, in_=bk[:, :], func=AF.Gelu),
                     reads=[br], writes=[obr])
                b.dma("sp", obd, self.uT[j, :, blk * 512:(blk + 1) * 512], ob, reads=[obr],
                      writes=[self.dr("uT", j, blk)])
        if pstop <= 4:
            return
        for c in range(4):
            wt, wr = self.wnext()
            for i in range(NB):
                bk, br = self.bank()
                fns = []
                for jj in range(4):
                    t = 4 * i + jj
                    for kc in range(KC):
                        fns.append(lambda e, jj=jj, t=t, kc=kc, bk=bk, wt=wt: e.matmul(
                            bk[:, jj * 128:(jj + 1) * 128], lhsT=A[:, kc, t * 128:(t + 1) * 128], rhs=wt[:, kc, :],
                            start=(kc == 0), stop=(kc == KC - 1)))
                b.group("pe", fns, reads=[wr, self.Ares[i]], writes=[br])
                of, ofr, ofd = self.of32.next()
                b.op("act", lambda e, of=of, bk=bk: e.activation(out=of, in_=bk[:, :], func=AF.Gelu),
                     reads=[br], writes=[ofr])
                b.dma("sp", ofd, self.vg[:, 4 * i:4 * i + 4, c * 128:(c + 1) * 128],
                      of.rearrange("p (a b) -> p a b", a=4), reads=[ofr], writes=[self.dr("vg", c, i)])

    def phase_attn(self, l):
        b = self.b
        A = self.A_bf
        scale = 1.0 / float(np.sqrt(128.0))
        qv = self.bview(0, [128, 4, T], BF16)
        kv = self.bview(5120, [128, T], BF16)
        vv = self.bview(6400, [128, NT, 256], BF16)
        ev = [self.bview(8960 + i * 256, [128, 512], BF16) for i in range(6)]
        self.b_phase()
        e_ring = Ring(b, "att_e", ev, mk=self)
        qr, kr, vr = self.newB(), self.newB(), self.newB()
        b.dma("sp", self.mds(), vv, self.V, reads=[self.dr("V", i2) for i2 in range(NT // 2)], writes=[vr])
        for g in range(2):
            b.dma("sp", self.mds(), qv, self.qT[4 * g:4 * g + 4].rearrange("h p t -> p h t"),
                  reads=[self.dr("qk", h, blk) for h in range(4 * g, 4 * g + 4) for blk in range(NB)], writes=[qr])
            b.dma("sp", self.mds(), kv, self.kT[g],
                  reads=[self.dr("qk", 8 + g, blk) for blk in range(NB)], writes=[kr])
            for t in range(NT):
                kts = [kt for kt in (t - 1, t, t + 1) if 0 <= kt < NT]
                qrhs = qv[:, :, t * 128:(t + 1) * 128]
                es = []
                for kt in kts:
                    sb_, sbr = self.bank(0, 6)
                    fns = [lambda e, sb_=sb_, kt=kt, qrhs=qrhs: e.matmul(
                        sb_[:, :], lhsT=kv[:, kt * 128:(kt + 1) * 128], rhs=qrhs, start=True, stop=(kt == t))]
                    if kt != t:
                        mi = 0 if kt < t else 1
                        fns.append(lambda e, sb_=sb_, mi=mi: e.matmul(
                            sb_[:, :], lhsT=self.ident_bf[:, :], rhs=self.mask_bf[:, mi, :], start=False, stop=True))
                    b.group("pe", fns, reads=[qr, kr, self.k_res], writes=[sbr])
                    ee, eer = e_ring.next()
                    b.op("act", lambda e, ee=ee, sb_=sb_: e.activation(out=ee, in_=sb_[:, :], func=AF.Exp, scale=scale),
                         reads=[sbr], writes=[eer])
                    es.append((kt, ee, eer))
                dn, dnr = self.banks[6], self.bres[6]
                ob_, obr_ = self.banks[7], self.bres[7]
                n = len(es)
                b.group("pe", [lambda e, i=i, ee=ee: e.matmul(dn[:, :], lhsT=self.ones_bf[:, :], rhs=ee,
                                                              start=(i == 0), stop=(i == n - 1))
                               for i, (kt, ee, eer) in enumerate(es)],
                        reads=[x[2] for x in es] + [self.k_res], writes=[dnr])
                b.group("pe", [lambda e, i=i, ee=ee, kt=kt: e.matmul(ob_[:, :], lhsT=vv[:, kt, g * 128:(g + 1) * 128], rhs=ee,
                                                                     start=(i == 0), stop=(i == n - 1))
                               for i, (kt, ee, eer) in enumerate(es)],
                        reads=[x[2] for x in es] + [vr], writes=[obr_])
                ds_, dsr = self.small.next()
                for hh in range(4):
                    h = 4 * g + hh
                    b.op("dve", lambda e, hh=hh, h=h, ds_=ds_: e.tensor_scalar(
                        out=ds_[:, hh * 128:(hh + 1) * 128], in0=dn[:, hh * 128:(hh + 1) * 128],
                        scalar1=self.esink[:, h:h + 1], scalar2=None, op0=ALU.add),
                        reads=[dnr, self.pl_res], writes=[dsr])
                b.op("dve", lambda e, ds_=ds_: e.reciprocal(out=ds_, in_=ds_), reads=[dsr], writes=[dsr])
                b.op("dve", lambda e, ds_=ds_, t=t, g=g: e.tensor_tensor(
                    out=A[:, 4 * g:4 * g + 4, t * 128:(t + 1) * 128],
                    in0=ob_[:, :].rearrange("p (a b) -> p a b", a=4),
                    in1=ds_.rearrange("p (a b) -> p a b", a=4), op=ALU.mult),
                    reads=[obr_, dsr], writes=[self.Ares[t // 4]])

    def phase_conv(self, l):
        b = self.b
        A = self.A_bf
        cb = [[self.bview((i * 4 + j) * 544, [128, 544], F32) for j in range(4)] for i in range(2)]
        yb = [[self.bview(4352 + (i * 4 + j) * 512, [128, 512], F32) for j in range(4)] for i in range(2)]
        ysq = [self.bview(8448 + i * 256, [128, 512], BF16) for i in range(2)]
        self.b_phase()
        cres = [[self.newB() for _ in range(4)] for _ in range(2)]
        yres = [[self.newB() for _ in range(4)] for _ in range(2)]
        ysq_ring = Ring(b, "cv_sq", ysq, mk=self)
        yh_ring = Ring(b, "cv_yh", [self.bview(8960, [128, 512], BF16)], mk=self)
        yl_ring = Ring(b, "cv_yl", [self.bview(9216, [128, 512], BF16)], mk=self)
        cds = self.cv_ds
        ci = 0
        wo, _ = self.coff["dww"]
        for blk in range(NB):
            par = blk % 2
            lo = blk * 512 - 15
            hi = blk * 512 + 512 + 15
            for j in range(4):
                c_, cr = cb[par][j], cres[par][j]
                slo, shi = max(lo, 0), min(hi, T)
                if lo < 0:
                    b.op("pool", lambda e, c_=c_: e.memset(c_[:, 0:15], 0.0), writes=[cr])
                if hi > T:
                    b.op("pool", lambda e, c_=c_: e.memset(c_[:, 527:542], 0.0), writes=[cr])
                rd = [self.dr("cT", j, bb) for bb in range(max(blk - 1, 0), min(blk + 2, NB))]
                b.dma("sp", cds[ci % 4], c_[:, slo - lo:shi - lo], self.cT[j, :, slo:shi], reads=rd, writes=[cr])
                ci += 1
            for j in range(4):
                c_, cr = cb[par][j], cres[par][j]
                y_, yr = yb[par][j], yres[par][j]
                w0 = wo + (l * 4 + j) * 31
                bo, _ = self.coff["dwb"]
                b.op("dve", lambda e, y_=y_, c_=c_, w0=w0, j=j: e.tensor_scalar(
                    out=y_, in0=c_[:, 0:512], scalar1=self.CS[:, w0:w0 + 1],
                    scalar2=self.CS[:, bo + l * 4 + j: bo + l * 4 + j + 1], op0=ALU.mult, op1=ALU.add),
                    reads=[cr, self.cs_res], writes=[yr])
                for k in range(1, 31):
                    b.op("dve", lambda e, y_=y_, c_=c_, w0=w0, k=k: e.scalar_tensor_tensor(
                        out=y_, in0=c_[:, k:k + 512], scalar=self.CS[:, w0 + k:w0 + k + 1], in1=y_,
                        op0=ALU.mult, op1=ALU.add), reads=[cr, yr, self.cs_res], writes=[yr])
            sm, smr = self.banks[4], self.bres[4]
            s2, s2r = self.banks[5], self.bres[5]
            for j in range(4):
                yh, yhr = yh_ring.next()
                yl, ylr = yl_ring.next()
                b.op("dve", lambda e: e.tensor_copy(out=yh, in_=yb[par][j]), reads=[yres[par][j]], writes=[yhr])
                b.op("dve", lambda e: e.tensor_tensor(out=yl, in0=yb[par][j], in1=yh, op=ALU.subtract),
                     reads=[yres[par][j], yhr], writes=[ylr])
                b.group("pe", [lambda e: e.matmul(sm[:, :], lhsT=self.ones_bf[:, :], rhs=yh, start=(j == 0), stop=False),
                               lambda e: e.matmul(sm[:, :], lhsT=self.ones_bf[:, :], rhs=yl, start=False, stop=(j == 3))],
                        reads=[yhr, ylr, self.k_res], writes=[smr])
                sq, sqr = ysq_ring.next()
                b.op("act", lambda e: e.activation(out=sq, in_=yb[par][j], func=AF.Square),
                     reads=[yres[par][j]], writes=[sqr])
                b.group("pe", [lambda e: e.matmul(s2[:, :], lhsT=self.ones_bf[:, :], rhs=sq,
                                                  start=(j == 0), stop=(j == 3))],
                        reads=[sqr, self.k_res], writes=[s2r])
            mean, mr = self.small.next()
            rstd, rr = self.small.next()
            b.op("dve", lambda e, mean=mean: e.tensor_scalar(out=mean, in0=sm[:, :], scalar1=1.0 / 512, scalar2=None,
                                                             op0=ALU.mult), reads=[smr], writes=[mr])
            tf, tfr = self.tmpf.next()
            b.op("dve", lambda e, tf=tf, mean=mean: e.tensor_tensor(out=tf, in0=mean, in1=mean, op=ALU.mult),
                 reads=[mr], writes=[tfr])
            b.op("dve", lambda e, rstd=rstd, tf=tf: e.scalar_tensor_tensor(
                out=rstd, in0=s2[:, :], scalar=1.0 / 512, in1=tf, op0=ALU.mult, op1=ALU.subtract),
                reads=[s2r, tfr], writes=[rr])
            self.rsqrt_eps(rstd, rr)
            go, _ = self.coff["clg"]
            bo2, _ = self.coff["clb"]
            for j in range(4):
                y_, yr = yb[par][j], yres[par][j]
                tf, tfr = self.tmpf.next()
                b.op("dve", lambda e, tf=tf, y_=y_, mean=mean: e.tensor_tensor(out=tf, in0=y_, in1=mean, op=ALU.subtract),
                     reads=[yr, mr], writes=[tfr])
                b.op("dve", lambda e, tf=tf, rstd=rstd: e.tensor_tensor(out=tf, in0=tf, in1=rstd, op=ALU.mult),
                     reads=[tfr, rr], writes=[tfr])
                b.op("dve", lambda e, tf=tf, j=j: e.tensor_scalar(
                    out=tf, in0=tf, scalar1=self.CS[:, go + l * 4 + j: go + l * 4 + j + 1],
                    scalar2=self.CS[:, bo2 + l * 4 + j: bo2 + l * 4 + j + 1], op0=ALU.mult, op1=ALU.add),
                    reads=[tfr, self.cs_res], writes=[tfr])
                b.op("act", lambda e, tf=tf, j=j, blk=blk: e.activation(
                    out=A[:, 8 + j, blk * 512:(blk + 1) * 512], in_=tf, func=AF.Silu),
                    reads=[tfr], writes=[self.Ares[blk]])

    def phase_sgu(self, l):
        b = self.b
        A = self.A_bf
        ub = self.bview(0, [128, 4, T], BF16)
        vgb = [self.bview(5120 + i * 2048, [128, 4, 512], F32) for i in range(2)]
        vnb = [self.bview(9216 + i * 1024, [128, 4, 512], BF16) for i in range(2)]
        st6 = self.bview(11264, [128, 8], F32)
        mv = self.bview(11272, [128, 8], F32)
        self.b_phase()
        vg_ring = Ring(b, f"sg_v{l}", vgb, ndsem=self.ds_p, mk=self)
        vn_ring = Ring(b, "sg_n", vnb, mk=self)
        ur = self.newB()
        sres = self.newB()
        b.dma("sp", self.mds(), ub, self.uT.rearrange("j p t -> p j t"),
              reads=[self.dr("uT", j, blk) for j in range(4) for blk in range(NB)], writes=[ur])
        lng = self.PL[:, 0:512]
        lnb = self.PL[:, 512:1024]
        bsb = self.PL[:, 1024:1536].rearrange("p (a b) -> p a b", a=4)
        for i in range(NB):
            vgt, vgr, vgd = vg_ring.next()
            b.dma("sp", vgd, vgt, self.vg[:, 4 * i:4 * i + 4, :],
                  reads=[self.dr("vg", c, i) for c in range(4)], writes=[vgr])
            vn, vnr = vn_ring.next()
            for jj in range(4):
                b.op("dve", lambda e, vgt=vgt, jj=jj: e.bn_stats(out=st6[:, 0:6], in_=vgt[:, jj, :]),
                     reads=[vgr], writes=[sres])
                b.op("dve", lambda e: e.bn_aggr(out=mv[:, 0:2], in_=st6[:, 0:6]), reads=[sres], writes=[sres])
                self.rsqrt_eps(mv[:, 1:2], sres)
                b.op("dve", lambda e, vgt=vgt, jj=jj: e.tensor_scalar(
                    out=vgt[:, jj, :], in0=vgt[:, jj, :], scalar1=mv[:, 0:1], scalar2=mv[:, 1:2],
                    op0=ALU.subtract, op1=ALU.mult), reads=[vgr, sres], writes=[vgr])
                b.op("dve", lambda e, vgt=vgt, jj=jj: e.tensor_tensor(out=vgt[:, jj, :], in0=vgt[:, jj, :], in1=lng, op=ALU.mult),
                     reads=[vgr, self.pl_res], writes=[vgr])
                b.op("dve", lambda e, vgt=vgt, vn=vn, jj=jj: e.tensor_tensor(out=vn[:, jj, :], in0=vgt[:, jj, :], in1=lnb, op=ALU.add),
                     reads=[vgr, self.pl_res], writes=[vnr])
            for h in range(4):
                bk, br = self.bank(0, 4)
                b.group("pe", [lambda e, jj=jj, h=h, bk=bk, vn=vn: e.matmul(
                    bk[:, jj * 128:(jj + 1) * 128], lhsT=vn[:, jj, h * 128:(h + 1) * 128], rhs=self.wsT_bf[:, h, :],
                    start=True, stop=True) for jj in range(4)], reads=[vnr, self.pl_res], writes=[br])
                tf, tfr = self.tmpf.next()
                b.op("dve", lambda e, tf=tf, bk=bk, h=h: e.tensor_tensor(
                    out=tf.rearrange("p (a b) -> p a b", a=4), in0=bk[:, :].rearrange("p (a b) -> p a b", a=4),
                    in1=bsb[:, h:h + 1, :].to_broadcast([128, 4, 128]), op=ALU.add),
                    reads=[br, self.pl_res], writes=[tfr])
                b.op("dve", lambda e, tf=tf, h=h, i=i: e.tensor_tensor(
                    out=A[:, 12 + h, i * 512:(i + 1) * 512], in0=tf, in1=ub[:, h, i * 512:(i + 1) * 512], op=ALU.mult),
                    reads=[tfr, ur], writes=[self.Ares[i]])

    def residual_epilogue(self, bk, br, j, blk):
        b = self.b
        xi, xir, xid = self.xin.next()
        sl = self.xT[2 * blk:2 * blk + 2, :, j, :].rearrange("h p t -> p h t")
        xrs = [self.xres[j][2 * blk], self.xres[j][2 * blk + 1]]
        b.dma("sp", xid, xi.rearrange("p (h t) -> p h t", h=2), sl, reads=xrs, writes=[xir])
        xo, xor_, xod = self.xout.next()
        b.op("dve", lambda e, xo=xo, xi=xi, bk=bk: e.tensor_tensor(out=xo, in0=bk[:, :], in1=xi, op=ALU.add),
             reads=[br, xir], writes=[xor_])
        b.dma("sp", xod, sl, xo.rearrange("p (h t) -> p h t", h=2), reads=[xor_], writes=xrs)

    def phase_wout(self, l):
        b = self.b
        for j in range(KC):
            wt, wr = self.wnext()
            for blk in range(NB):
                bk, br = self.bank()
                self.gemm_fm(wt, wr, KC, self.A_bf, self.Ares[blk], blk, bk, br)
                self.residual_epilogue(bk, br, j, blk)

    def phase_ffn(self, l):
        b = self.b
        act = self.bview(0, [128, FPP, T], BF16)
        self.b_phase()
        actres = [self.newB() for _ in range(NB)]
        for p in range(NPASS):
            for fi in range(FPP):
                wg, wgr = self.wnext(ahead=2)
                wu, wur = self.wnext(ahead=1)
                for blk in range(NB):
                    pg, pgr = self.bank()
                    self.gemm_fm(wg, wgr, KC, self.A_bf, self.Ares[blk], blk, pg, pgr)
                    pu, pur = self.bank()
                    self.gemm_fm(wu, wur, KC, self.A_bf, self.Ares[blk], blk, pu, pur)
                    tf, tfr = self.tmpf.next()
                    b.op("act", lambda e, tf=tf, pg=pg: e.activation(out=tf, in_=pg[:, :], func=AF.Silu),
                         reads=[pgr], writes=[tfr])
                    b.op("dve", lambda e, tf=tf, pu=pu, fi=fi, blk=blk: e.tensor_tensor(
                        out=act[:, fi, blk * 512:(blk + 1) * 512], in0=pu[:, :], in1=tf, op=ALU.mult),
                        reads=[pur, tfr], writes=[actres[blk]])
            for j in range(KC):
                wt, wr = self.wnext()
                for blk in range(NB):
                    bk, br = self.bank()
                    self.gemm_fm(wt, wr, FPP, act, actres[blk], blk, bk, br)
                    self.residual_epilogue(bk, br, j, blk)

    def phase_final(self):
        b = self.b
        go, _ = self.coff["gf"]
        gv = self.CS[:, go:go + 16]
        ident = self.cview("ident")
        self.b_phase()
        xs_ring = Ring(b, "fin_x", [self.bview(i * 4096, [128, KC, 256], F32) for i in range(2)], ndsem=self.ds_p, mk=self)
        sq_ring = Ring(b, "fin_q", [self.bview(8192 + i * 2048, [128, KC, 256], BF16) for i in range(1)], mk=self)
        rs_ring = Ring(b, "fin_r", [self.bview(10240 + i * 256, [128, 256], F32) for i in range(1)], mk=self)
        ot_views = [self.A[:, i * 2048:(i + 1) * 2048] for i in range(2)]
        ot_ring = Ring(b, "fin_o", ot_views, ndsem=self.ds_q)
        a_all = list(self.Ares)
        self.final_evs = []
        for hb in range(T // 256):
            blk = hb // 2
            t0 = hb * 256
            xs, xr, xd = xs_ring.next()
            b.dma("sp", xd, xs, self.xT[hb], reads=[self.xres[kc][hb] for kc in range(KC)], writes=[xr])
            sq, sqr = sq_ring.next()
            b.op("act", lambda e, sq=sq, xs=xs: e.activation(out=sq, in_=xs, func=AF.Square), reads=[xr], writes=[sqr])
            bk, br = self.bank(4, 6)
            fns = [lambda e, kc=kc, bk=bk, sq=sq: e.matmul(bk[:, 0:256], lhsT=self.ones_bf[:, :], rhs=sq[:, kc, :],
                                                          start=(kc == 0), stop=(kc == KC - 1)) for kc in range(KC)]
            b.group("pe", fns, reads=[sqr, self.k_res], writes=[br])
            rs, rsr = rs_ring.next()
            b.op("dve", lambda e, rs=rs, bk=bk: e.tensor_scalar(out=rs, in0=bk[:, 0:256], scalar1=1.0 / D, scalar2=None,
                                                               op0=ALU.mult), reads=[br], writes=[rsr])
            self.rsqrt_eps(rs, rsr)
            for kc in range(KC):
                b.op("dve", lambda e, kc=kc, xs=xs, rs=rs: e.scalar_tensor_tensor(
                    out=xs[:, kc, :], in0=xs[:, kc, :], scalar=gv[:, kc:kc + 1], in1=rs,
                    op0=ALU.mult, op1=ALU.mult), reads=[xr, rsr, self.cs_res], writes=[xr])
            for tt in range(2):
                ot, otr, otd = ot_ring.next()
                for g in range(4):
                    pb, pbr = self.bank(0, 4)
                    fns = [lambda e, j=j, g=g, pb=pb, xs=xs, tt=tt: e.transpose(
                        out=pb[:, j * 128:(j + 1) * 128], in_=xs[:, 4 * g + j, tt * 128:(tt + 1) * 128],
                        identity=ident) for j in range(4)]
                    b.group("pe", fns, reads=[xr, self.cs_res], writes=[pbr])
                    if g % 2 == 0:
                        b.op("dve", lambda e, ot=ot, pb=pb, g=g: e.tensor_copy(out=ot[:, g * 512:(g + 1) * 512], in_=pb[:, :]),
                             reads=[pbr], writes=[otr] + a_all)
                    else:
                        b.op("act", lambda e, ot=ot, pb=pb, g=g: e.activation(out=ot[:, g * 512:(g + 1) * 512], in_=pb[:, :], func=AF.Copy),
                             reads=[pbr], writes=[otr] + a_all)
                    a_all = []
                tok0 = t0 + tt * 128
                ev = b.dma("sp", otd, self.out[tok0:tok0 + 128, :], ot, reads=[otr], writes=[Res()])
                self.final_evs.append(ev)

    def finish(self):
        b = self.b
        for d in b.all_ds:
            if d.count:
                b._wait("sp", (d.key, d.count))
        b.streams["sp"].append(("ins", lambda e: e.nop(), None, 0))

    def weight_tiles(self):
        tiles = []
        for l in range(self.L):
            for i in range(NWIN):
                tiles.append((self.w_in[l, i], KC))
            for i in range(KC):
                tiles.append((self.w_out[l, i], KC))
            for p in range(NPASS):
                for fi in range(FPP):
                    f = p * FPP + fi
                    tiles.append((self.w_gate[l, f], KC))
                    tiles.append((self.w_up[l, f], KC))
                for j in range(KC):
                    tiles.append((self.w_down[l, p, j], FPP))
        return tiles

    def build(self, upto=None):
        order = ["in", "norm1", "proj", "attn", "conv", "sgu", "wout", "norm2", "ffn"]
        stop = len(order) if upto is None else order.index(upto) + 1
        with contextlib.ExitStack() as st:
            self.setup(st)
            self.load_consts()
            self.wstream_init(self.weight_tiles())
            if stop >= 1:
                self.phase_in()
            for l in range(self.L):
                self.load_layer_consts(l)
                steps = [None, lambda: self.phase_norm("g1", l), lambda: self.phase_proj(l), lambda: self.phase_attn(l),
                         lambda: self.phase_conv(l), lambda: self.phase_sgu(l), lambda: self.phase_wout(l),
                         lambda: self.phase_norm("g2", l), lambda: self.phase_ffn(l)]
                for i in range(1, min(stop, len(order))):
                    if self.debug and l == 0 and order[i] == "wout":
                        self.b.dma("sp", self.mds(), self.dbgA, self.A_bf, reads=self.Ares, writes=[Res()])
                    steps[i]()
            if upto is None:
                self.phase_final()
            self.finish()
            self.b.replay()
        return self.nc


def _win_perm():
    cols = []
    for j in range(4):
        cols += list(range(1536 + j * 128, 1536 + (j + 1) * 128))
        cols += list(range(2048 + j * 128, 2048 + (j + 1) * 128))
    cols += list(range(0, 1024))
    cols += list(range(1024, 1280))
    cols += list(range(1280, 1536))
    cols += list(range(2560, 3072))
    cols += list(range(3072, 3584))
    return np.array(cols)


def _tile_kc(w):
    L, K, N = w.shape
    return np.ascontiguousarray(w.reshape(L, K // 128, 128, N // 128, 128).transpose(0, 3, 2, 1, 4))


def prep_weights(w_in, w_out, w_gate, w_up, w_down):
    L = w_in.shape[0]
    d = {}
    d["w_in"] = _tile_kc(np.asarray(w_in)[:, :, _win_perm()])
    d["w_out"] = _tile_kc(np.asarray(w_out))
    d["w_gate"] = _tile_kc(np.asarray(w_gate))
    d["w_up"] = _tile_kc(np.asarray(w_up))
    wd = np.asarray(w_down).reshape(L, NPASS, FPP, 128, KC, 128)
    d["w_down"] = np.ascontiguousarray(wd.transpose(0, 1, 4, 3, 2, 5))
    return d


def prep_consts(L, mix_norm_g, ffn_norm_g, final_norm_g, conv_dw_w, conv_dw_b, conv_ln_g, conv_ln_b,
                sink, sgu_ln_g, sgu_ln_b, sgu_w, sgu_b):
    off, words = _const_layout(L)
    cst = np.zeros((128, words), np.float32)

    def put(name, arr):
        o, n = off[name]
        assert arr.shape == (128, n), (name, arr.shape, n)
        cst[:, o:o + n] = arr

    f = np.float32
    put("g1", np.asarray(mix_norm_g, f).reshape(L, 16, 128).transpose(2, 0, 1).reshape(128, L * 16))
    put("g2", np.asarray(ffn_norm_g, f).reshape(L, 16, 128).transpose(2, 0, 1).reshape(128, L * 16))
    put("gf", np.asarray(final_norm_g, f).reshape(16, 128).T)
    put("dww", np.asarray(conv_dw_w, f).reshape(L, 31, 4, 128).transpose(3, 0, 2, 1).reshape(128, L * 4 * 31))
    put("dwb", np.asarray(conv_dw_b, f).reshape(L, 4, 128).transpose(2, 0, 1).reshape(128, L * 4))
    put("clg", np.asarray(conv_ln_g, f).reshape(L, 4, 128).transpose(2, 0, 1).reshape(128, L * 4))
    put("clb", np.asarray(conv_ln_b, f).reshape(L, 4, 128).transpose(2, 0, 1).reshape(128, L * 4))
    put("sink", np.broadcast_to(np.asarray(sink, f).reshape(1, L * 8), (128, L * 8)))
    put("ident", np.eye(128, dtype=f))
    rm = np.zeros((128, 32), f)
    for m in range(32):
        rm[(m + 16) % 32, m] = 1.0
    put("rm", rm)
    kk = np.arange(128)[:, None]
    qq = np.arange(128)[None, :]
    mL = np.where(kk >= qq, 0.0, NEG).astype(f)
    mR = np.where(kk <= qq, 0.0, NEG).astype(f)
    msk = np.stack([np.tile(mL, (1, 4)), np.tile(mR, (1, 4))], axis=1).astype(f)
    plc = np.zeros((L, 128, PL_WORDS), f)
    plc[:, :, 0:512] = np.asarray(sgu_ln_g, f)[:, None, :]
    plc[:, :, 512:1024] = np.asarray(sgu_ln_b, f)[:, None, :]
    plc[:, :, 1024:1536] = np.asarray(sgu_b, f).reshape(L, 1, 512)
    plc[:, :, 1536:2048] = np.asarray(sgu_w, f).transpose(0, 3, 1, 2).reshape(L, 128, 512)
    return cst, plc, msk


def rope_table(s0):
    pos = np.arange(s0, s0 + T, dtype=np.float32)
    inv = (np.float32(500000.0) ** (-np.arange(0, 32, 2, dtype=np.float32) / np.float32(32))).astype(np.float32)
    ang = (pos[:, None] * inv[None, :]).astype(np.float32)
    c = np.cos(ang).astype(np.float32).T
    s = np.sin(ang).astype(np.float32).T
    tab = np.zeros((32, 2, T), np.float32)
    tab[0:16, 0] = c
    tab[16:32, 0] = c
    tab[0:16, 1] = -s
    tab[16:32, 1] = s
    return tab


_PROG = {}


def get_prog(L, debug=False):
    key = (L, debug)
    if key not in _PROG:
        _PROG[key] = MK(L, debug=debug).build()
    return _PROG[key]


def core_starts():
    return [(c // 2, 0 if c % 2 == 0 else SEQ - T) for c in range(8)]


def kernel(x, mix_norm_g, w_in, sink, conv_dw_w, conv_dw_b, conv_ln_g, conv_ln_b,
           sgu_ln_g, sgu_ln_b, sgu_w, sgu_b, w_out, ffn_norm_g, w_gate, w_up, w_down,
           final_norm_g):
    x = np.asarray(x, np.float32)
    L = DEPTH
    wd = prep_weights(np.asarray(w_in, np.float32), np.asarray(w_out, np.float32), np.asarray(w_gate, np.float32),
                      np.asarray(w_up, np.float32), np.asarray(w_down, np.float32))
    cst, plc, msk = prep_consts(L, mix_norm_g, ffn_norm_g, final_norm_g, conv_dw_w, conv_dw_b, conv_ln_g, conv_ln_b,
                           sink, sgu_ln_g, sgu_ln_b, sgu_w, sgu_b)
    nc = get_prog(L)
    in_maps = []
    for (bi, s0) in core_starts():
        m = dict(wd)
        m["x_in"] = np.ascontiguousarray(x[bi, s0:s0 + T, :])
        m["cst"] = cst
        m["plc"] = plc
        m["msk"] = msk
        m["rope"] = rope_table(s0)
        in_maps.append(m)
    res = run_bass_kernel_spmd(nc, in_maps, core_ids=list(range(8)))
    out = np.empty((BATCH, SEQ, D), np.float32)
    for c, (bi, s0) in enumerate(core_starts()):
        o = res.results[c]["out"]
        if s0 == 0:
            out[bi, 0:OWN] = o[0:OWN]
        else:
            out[bi, SEQ - OWN:SEQ] = o[T - OWN:T]
    return out
```

```python
import contextlib
import os
import types
from math import prod

import numpy as np
import concourse.bass as bass
import concourse.mybir as mybir
from concourse.bass_utils import run_bass_kernel_spmd

F32 = mybir.dt.float32
BF16 = mybir.dt.bfloat16
AF = mybir.ActivationFunctionType
ALU = mybir.AluOpType

D = 2048
SEQ = 4096
BATCH = 4
DEPTH = 4
T = 2560
OWN = 2048
NT = T // 128
NB = T // 512
KC = D // 128
DFF = 5632
NF = DFF // 128
NPASS = 4
FPP = NF // NPASS
NWIN = 28
EPS = 1e-6
NEG = -30000.0

ENGS = ["pe", "act", "dve", "pool", "sp"]
SAME_ENGINE_SYNC = True


def _freeze(fn):
    if fn.__closure__ is None:
        return fn
    cells = []
    for c in fn.__closure__:
        try:
            cells.append(types.CellType(c.cell_contents))
        except ValueError:
            cells.append(c)
    g = types.FunctionType(fn.__code__, fn.__globals__, fn.__name__, fn.__defaults__, tuple(cells))
    g.__kwdefaults__ = fn.__kwdefaults__
    return g


class Res:
    __slots__ = ("w", "r")

    def __init__(self):
        self.w = None
        self.r = []


class DSem:
    __slots__ = ("key", "count")

    def __init__(self, key):
        self.key = key
        self.count = 0


class Builder:
    def __init__(self, nc, stack):
        self.nc = nc
        self.stack = stack
        self.streams = {e: [] for e in ENGS}
        self.sems = []
        self.cnt = {}
        self.waited = {e: {} for e in ENGS}
        self.esem = {}
        for e in ENGS:
            self.esem[e] = self.new_sem("e_" + e)
            self.cnt[e] = 0
        self.n_ins = 0

    def new_sem(self, name):
        h = self.stack.enter_context(self.nc.semaphore(name))
        self.sems.append(h)
        return len(self.sems) - 1

    def dsem_pool(self, name, n):
        ds = [DSem(self.new_sem(f"{name}{i}")) for i in range(n)]
        self.all_ds = getattr(self, "all_ds", []) + ds
        return ds

    def sb(self, name, shape, dtype):
        return self.stack.enter_context(self.nc.sbuf_tensor(name, list(shape), dtype))[:]

    def ps(self, name, shape, dtype):
        return self.stack.enter_context(self.nc.psum_tensor(name, list(shape), dtype))[:]

    def _wait(self, eng, ev):
        if ev is None:
            return
        s, v = ev
        if s == self.esem[eng] and (eng == "pe" or not SAME_ENGINE_SYNC):
            return
        if self.waited[eng].get(s, 0) >= v:
            return
        self.waited[eng][s] = v
        self.streams[eng].append(("wait", s, v))

    def _deps(self, eng, reads, writes):
        for r in reads:
            self._wait(eng, r.w)
        for w in writes:
            self._wait(eng, w.w)
            for ev in w.r:
                self._wait(eng, ev)

    def _mark(self, ev, reads, writes):
        for r in reads:
            r.r = [e for e in r.r if e[0] != ev[0]]
            r.r.append(ev)
        for w in writes:
            w.w = ev
            w.r = []

    def op(self, eng, fn, reads=(), writes=()):
        self._deps(eng, reads, writes)
        self.cnt[eng] += 1
        ev = (self.esem[eng], self.cnt[eng])
        self.streams[eng].append(("ins", _freeze(fn), ev[0], 1))
        self._mark(ev, reads, writes)
        self.n_ins += 1
        return ev

    def group(self, eng, fns, reads=(), writes=()):
        self._deps(eng, reads, writes)
        for fn in fns[:-1]:
            self.streams[eng].append(("ins", _freeze(fn), None, 0))
        self.cnt[eng] += 1
        ev = (self.esem[eng], self.cnt[eng])
        self.streams[eng].append(("ins", _freeze(fns[-1]), ev[0], 1))
        self._mark(ev, reads, writes)
        self.n_ins += len(fns)
        return ev

    def dma(self, q, dsem, out, in_, reads=(), writes=()):
        self._deps(q, reads, writes)
        if dsem.count:
            self._wait(q, (dsem.key, dsem.count))
        dsem.count += 16
        ev = (dsem.key, dsem.count)
        self.streams[q].append(("ins", lambda e, o=out, i=in_: e.dma_start(out=o, in_=i), ev[0], 16))
        self._mark(ev, reads, writes)
        self.n_ins += 1
        return ev

    def replay(self):
        nc = self.nc
        sems = self.sems
        streams = self.streams

        def run(e, st):
            for it in st:
                if it[0] == "wait":
                    e.wait_ge(sems[it[1]], it[2])
                else:
                    ins = it[1](e)
                    if it[2] is not None:
                        ins.then_inc(sems[it[2]], it[3])

        with nc.Block() as block:
            @block.tensor
            def _(e):
                run(e, streams["pe"])

            @block.scalar
            def _(e):
                run(e, streams["act"])

            @block.vector
            def _(e):
                run(e, streams["dve"])

            @block.gpsimd
            def _(e):
                run(e, streams["pool"])

            @block.sync
            def _(e):
                run(e, streams["sp"])


class Ring:
    def __init__(self, b, name, views, ndsem=0, mk=None):
        self.views = views
        self.res = [(mk.newB() if mk is not None else Res()) for _ in views]
        if ndsem and not isinstance(ndsem, int):
            self.ds = ndsem
            assert len(self.ds) >= len(views)
        else:
            self.ds = b.dsem_pool(name, len(views)) if ndsem else None
        self.i = 0

    def next(self):
        k = self.i % len(self.views)
        self.i += 1
        if self.ds:
            return self.views[k], self.res[k], self.ds[k]
        return self.views[k], self.res[k]


def _const_layout(L):
    off = {}
    o = 0
    for name, n in [("g1", L * 16), ("g2", L * 16), ("gf", 16), ("dww", L * 4 * 31),
                    ("dwb", L * 4), ("clg", L * 4), ("clb", L * 4), ("sink", L * 8),
                    ("ident", 128), ("rm", 32)]:
        off[name] = (o, n)
        o += n
    return off, o


PL_WORDS = 512 + 512 + 512 + 512


class MK:
    def __init__(self, L, debug=False, l0=0):
        self.L = L
        self.debug = debug
        nc = self.nc = bass.Bass("TRN2", target_bir_lowering=False)
        dk = "ExternalOutput" if debug else None

        def dram(name, shape, dt, kind=None):
            if kind is None:
                return nc.dram_tensor(name, list(shape), dt).ap()
            return nc.dram_tensor(name, list(shape), dt, kind=kind).ap()

        self.coff, self.cwords = _const_layout(L)
        self.x_in = dram("x_in", [T, D], F32, "ExternalInput")
        self.w_in = dram("w_in", [L, NWIN, 128, KC, 128], F32, "ExternalInput")
        self.w_out = dram("w_out", [L, KC, 128, KC, 128], F32, "ExternalInput")
        self.w_gate = dram("w_gate", [L, NF, 128, KC, 128], F32, "ExternalInput")
        self.w_up = dram("w_up", [L, NF, 128, KC, 128], F32, "ExternalInput")
        self.w_down = dram("w_down", [L, NPASS, KC, 128, FPP, 128], F32, "ExternalInput")
        self.cst = dram("cst", [128, self.cwords], F32, "ExternalInput")
        self.plc = dram("plc", [L, 128, PL_WORDS], F32, "ExternalInput")
        self.rope = dram("rope", [32, 2, T], F32, "ExternalInput")
        self.msk = dram("msk", [128, 2, 512], F32, "ExternalInput")
        self.out = dram("out", [T, D], F32, "ExternalOutput")
        self.xT = dram("xT", [T // 256, 128, KC, 256], F32, dk)
        self.qT = dram("qT", [8, 128, T], BF16, dk)
        self.kT = dram("kT", [2, 128, T], BF16, dk)
        self.V = dram("V", [128, NT, 256], BF16, dk)
        self.cT = dram("cT", [4, 128, T], F32, dk)
        self.uT = dram("uT", [4, 128, T], BF16, dk)
        self.vg = dram("vg", [128, NT, 512], F32, dk)
        if debug:
            self.dbgA = dram("dbgA", [128, KC, T], BF16, dk)

    def setup(self, st):
        nc = self.nc
        b = self.b = Builder(nc, st)
        L = self.L
        A_W = KC * T // 2
        B_W = 14080
        WS_W = 2 * 2048
        WB_W = 3 * 1024
        self.A = b.sb("A", [128, A_W], F32)
        self.Bm = b.sb("Bm", [128, B_W], F32)
        self.WS = b.sb("WS", [128, WS_W], F32)
        self.WB = b.sb("WB", [128, WB_W], F32)
        self.CS = b.sb("CS", [128, self.cwords], F32)
        self.PL = b.sb("PL", [128, PL_WORDS], F32)
        self.wsT_bf = b.sb("wsT_bf", [128, 4, 128], BF16)
        self.mask_bf = b.sb("mask_bf", [128, 2, 512], BF16)
        self.ones_bf = b.sb("ones_bf", [128, 128], BF16)
        self.rm_bf = b.sb("rm_bf", [128, 128], BF16)
        self.ident_bf = b.sb("ident_bf", [128, 128], BF16)
        self.esink = b.sb("esink", [128, 8], F32)
        self.A_bf = self.A[:, :].bitcast(BF16).rearrange("p (a b) -> p a b", a=KC)
        self.Ares = [Res() for _ in range(NB)]
        self.tmpf = Ring(b, "tmpf", [b.sb(f"tmpf{i}", [128, 512], F32) for i in range(2)])
        self.obf = Ring(b, "obf", [b.sb(f"obf{i}", [128, 512], BF16) for i in range(3)], ndsem=1)
        self.xin = Ring(b, "xin", [b.sb(f"xin{i}", [128, 512], F32) for i in range(4)], ndsem=1)
        self.xout = Ring(b, "xout", [b.sb(f"xout{i}", [128, 512], F32) for i in range(3)], ndsem=1)
        self.of32 = self.xout
        self.small = Ring(b, "small", [b.sb(f"small{i}", [128, 512], F32) for i in range(2)])
        self.banks = [b.ps(f"bank{i}", [128, 512], F32) for i in range(8)]
        self.bres = [Res() for _ in range(8)]
        self.bank_i = 0
        ws3 = self.WS[:, :].rearrange("p (s k c) -> p s k c", s=2, k=KC)
        wb3 = self.WB[:, :].bitcast(BF16).rearrange("p (s k c) -> p s k c", s=3, k=KC)
        self.wstage = [(ws3[:, i], Res(), d) for i, d in enumerate(b.dsem_pool("wld", 2))]
        self.wbf = [(wb3[:, i], Res()) for i in range(3)]
        self.misc_ds = b.dsem_pool("misc", 4)
        self.misc_i = 0
        self.xres = [[Res() for _ in range(T // 256)] for _ in range(KC)]
        self.dres = {}
        self.cs_res = Res()
        self.pl_res = Res()
        self.b_fence = []
        self.b_res = []
        self.cv_ds = b.dsem_pool("cvl", 4)
        self.ds_p = b.dsem_pool("dsp", 2)
        self.ds_q = b.dsem_pool("dsq", 2)

    def newB(self):
        r = Res()
        r.r = list(self.b_fence)
        self.b_res.append(r)
        return r

    def b_phase(self):
        best = {}
        for ev in self.b_fence:
            best[ev[0]] = max(best.get(ev[0], 0), ev[1])
        for r in self.b_res:
            for ev in ([r.w] if r.w else []) + r.r:
                best[ev[0]] = max(best.get(ev[0], 0), ev[1])
        self.b_fence = [(k, v) for k, v in best.items()]
        self.b_res = []

    def dr(self, *key):
        r = self.dres.get(key)
        if r is None:
            r = self.dres[key] = Res()
        return r

    def mds(self):
        d = self.misc_ds[self.misc_i % len(self.misc_ds)]
        self.misc_i += 1
        return d

    def bank(self, lo=0, hi=4):
        n = hi - lo
        k = lo + (self.bank_i % n)
        self.bank_i += 1
        return self.banks[k], self.bres[k]

    def cview(self, name):
        o, n = self.coff[name]
        return self.CS[:, o:o + n]

    def bview(self, off, shape, dtype):
        n = prod(shape[1:])
        nw = n // 2 if dtype == BF16 else n
        assert off + nw <= self.Bm.shape[1], (off, nw)
        ap = self.Bm[:, off:off + nw]
        if dtype == BF16:
            ap = ap.bitcast(BF16)
        if len(shape) == 3:
            ap = ap.rearrange("p (a b) -> p a b", a=shape[1])
        return ap

    def wstream_init(self, tiles):
        self.wtiles = tiles
        self.w_cast = 0
        self.w_used = 0
        self.w_loaded = 0
        for _ in range(2):
            self._wload()

    def _wload(self):
        n = self.w_loaded
        if n >= len(self.wtiles):
            return
        ap, kc = self.wtiles[n]
        sv, sr, ds = self.wstage[n % 2]
        self.b.dma("sp", ds, sv[:, 0:kc, :], ap, writes=[sr])
        self.w_loaded += 1

    def _wcast(self):
        n = self.w_cast
        if n >= len(self.wtiles):
            return
        _, kc = self.wtiles[n]
        sv, sr, _ = self.wstage[n % 2]
        bv, br = self.wbf[n % 3]
        self.b.op("pool", lambda e, o=bv[:, 0:kc, :], i=sv[:, 0:kc, :]: e.tensor_copy(out=o, in_=i),
                  reads=[sr], writes=[br])
        self.w_cast += 1
        self._wload()

    def wnext(self, ahead=1):
        n = self.w_used
        while self.w_cast <= min(n + ahead, len(self.wtiles) - 1):
            self._wcast()
        self.w_used += 1
        return self.wbf[n % 3]

    def rsqrt_eps(self, ap, res):
        b = self.b
        b.op("dve", lambda e: e.tensor_scalar(out=ap, in0=ap, scalar1=EPS, scalar2=None, op0=ALU.add),
             reads=[res], writes=[res])
        b.op("act", lambda e: e.activation(out=ap, in_=ap, func=AF.Sqrt), reads=[res], writes=[res])
        b.op("dve", lambda e: e.reciprocal(out=ap, in_=ap), reads=[res], writes=[res])

    def gemm_fm(self, wt, wr, kc_n, act, ares, blk, bank, bres, n=512):
        fns = []
        for kc in range(kc_n):
            fns.append(lambda e, kc=kc: e.matmul(bank[:, 0:n], lhsT=wt[:, kc, :],
                                                 rhs=act[:, kc, blk * 512:blk * 512 + n],
                                                 start=(kc == 0), stop=(kc == kc_n - 1)))
        self.b.group("pe", fns, reads=[wr, ares], writes=[bres])

    def load_consts(self):
        b = self.b
        b.dma("sp", self.mds(), self.CS[:, :], self.cst, writes=[self.cs_res])
        r = Res()
        b.op("pool", lambda e: e.memset(self.ones_bf[:, :], 1.0), writes=[r])
        b.op("pool", lambda e: e.memset(self.rm_bf[:, :], 0.0), writes=[r])
        b.op("pool", lambda e: e.tensor_copy(out=self.rm_bf[:, 0:32], in_=self.cview("rm")),
             reads=[self.cs_res], writes=[r])
        b.op("pool", lambda e: e.tensor_copy(out=self.ident_bf[:, :], in_=self.cview("ident")),
             reads=[self.cs_res], writes=[r])
        mtmp = self.bview(0, [128, 2, 512], F32)
        mr = self.newB()
        b.dma("sp", self.mds(), mtmp, self.msk, writes=[mr])
        b.op("pool", lambda e: e.tensor_copy(out=self.mask_bf[:, :, :], in_=mtmp), reads=[mr], writes=[r])
        self.k_res = r

    def phase_in(self):
        b = self.b
        xt_views = [self.bview(i * 2048, [128, 2048], F32) for i in range(2)]
        xo_views = [self.bview(4096 + i * 4096, [128, KC, 256], F32) for i in range(2)]
        self.b_phase()
        xt_ring = Ring(b, "pin_l", xt_views, ndsem=self.ds_p, mk=self)
        xo_ring = Ring(b, "pin_s", xo_views, ndsem=self.ds_q, mk=self)
        ident = self.cview("ident")
        for hb in range(T // 256):
            xo, xor_, xod = xo_ring.next()
            for tt in range(2):
                t = 2 * hb + tt
                xt, xr, xd = xt_ring.next()
                b.dma("sp", xd, xt, self.x_in[t * 128:(t + 1) * 128, :], writes=[xr])
                for g in range(4):
                    bk, br = self.bank()
                    fns = [lambda e, j=j: e.transpose(out=bk[:, j * 128:(j + 1) * 128],
                                                      in_=xt[:, (4 * g + j) * 128:(4 * g + j + 1) * 128],
                                                      identity=ident) for j in range(4)]
                    b.group("pe", fns, reads=[xr, self.cs_res], writes=[br])
                    src = bk[:, :].rearrange("p (a b) -> p a b", a=4)
                    dst = xo[:, 4 * g:4 * g + 4, tt * 128:(tt + 1) * 128]
                    if g % 2 == 0:
                        b.op("dve", lambda e: e.tensor_copy(out=dst, in_=src), reads=[br], writes=[xor_])
                    else:
                        b.op("act", lambda e: e.activation(out=dst, in_=src, func=AF.Copy), reads=[br], writes=[xor_])
            b.dma("sp", xod, self.xT[hb], xo, reads=[xor_], writes=[self.xres[kc][hb] for kc in range(KC)])

    def phase_norm(self, gname, l):
        b = self.b
        go, _ = self.coff[gname]
        gv = self.CS[:, go + l * 16: go + (l + 1) * 16]
        self.b_phase()
        xs_ring = Ring(b, f"nrm_x{gname}{l}", [self.bview(i * 4096, [128, KC, 256], F32) for i in range(2)], ndsem=self.ds_p, mk=self)
        sq_ring = Ring(b, "nrm_q", [self.bview(8192 + i * 2048, [128, KC, 256], BF16) for i in range(2)], mk=self)
        rs_ring = Ring(b, "nrm_r", [self.bview(12288 + i * 256, [128, 256], F32) for i in range(2)], mk=self)
        for hb in range(T // 256):
            blk = hb // 2
            t0 = hb * 256
            xs, xr, xd = xs_ring.next()
            b.dma("sp", xd, xs, self.xT[hb], reads=[self.xres[kc][hb] for kc in range(KC)], writes=[xr])
            sq, sqr = sq_ring.next()
            b.op("act", lambda e, sq=sq, xs=xs: e.activation(out=sq, in_=xs, func=AF.Square), reads=[xr], writes=[sqr])
            bk, br = self.bank(4, 6)
            fns = [lambda e, kc=kc, bk=bk, sq=sq: e.matmul(bk[:, 0:256], lhsT=self.ones_bf[:, :], rhs=sq[:, kc, :],
                                                          start=(kc == 0), stop=(kc == KC - 1)) for kc in range(KC)]
            b.group("pe", fns, reads=[sqr, self.k_res], writes=[br])
            rs, rsr = rs_ring.next()
            b.op("dve", lambda e, rs=rs, bk=bk: e.tensor_scalar(out=rs, in0=bk[:, 0:256], scalar1=1.0 / D, scalar2=None,
                                                               op0=ALU.mult), reads=[br], writes=[rsr])
            self.rsqrt_eps(rs, rsr)
            for kc in range(KC):
                b.op("dve", lambda e, kc=kc, xs=xs, rs=rs, t0=t0: e.scalar_tensor_tensor(
                    out=self.A_bf[:, kc, t0:t0 + 256], in0=xs[:, kc, :], scalar=gv[:, kc:kc + 1], in1=rs,
                    op0=ALU.mult, op1=ALU.mult), reads=[xr, rsr, self.cs_res], writes=[self.Ares[blk]])

    def load_layer_consts(self, l):
        b = self.b
        b.dma("sp", self.mds(), self.PL[:, :], self.plc[l], writes=[self.pl_res])
        b.op("pool", lambda e: e.tensor_copy(out=self.wsT_bf[:, :, :],
                                             in_=self.PL[:, 1536:2048].rearrange("p (a b) -> p a b", a=4)),
             reads=[self.pl_res], writes=[self.pl_res])
        so, _ = self.coff["sink"]
        b.op("act", lambda e: e.activation(out=self.esink[:, :], in_=self.CS[:, so + l * 8: so + l * 8 + 8], func=AF.Exp),
             reads=[self.cs_res], writes=[self.pl_res])

    def phase_proj(self, l):
        b = self.b
        A = self.A_bf
        ropev = self.bview(0, [128, 2 * T], F32)[0:32, :].rearrange("p (a b) -> p a b", a=2)
        self.b_phase()
        rope_r = self.newB()
        b.dma("sp", self.mds(), ropev, self.rope, writes=[rope_r])
        self.q32 = Ring(b, "q32", [self.bview(5120 + i * 512, [128, 512], F32) for i in range(2)], mk=self)
        self.t1 = Ring(b, "t1", [self.bview(6144 + i * 512, [128, 512], F32)[0:32, :] for i in range(2)], mk=self)
        self.t2 = Ring(b, "t2", [self.bview(7168 + i * 512, [128, 512], F32)[0:32, :] for i in range(2)], mk=self)
        qh_ring = Ring(b, "qh", [self.bview(8192 + i * 256, [128, 512], BF16) for i in range(2)], mk=self)
        ql_ring = Ring(b, "ql", [self.bview(8704 + i * 256, [128, 512], BF16) for i in range(2)], mk=self)
        rmv = self.rm_bf
        for j in range(4):
            wa, war = self.wnext(ahead=2)
            wg, wgr = self.wnext(ahead=1)
            for blk in range(NB):
                pa, par = self.bank()
                self.gemm_fm(wa, war, KC, A, self.Ares[blk], blk, pa, par)
                pg, pgr = self.bank()
                self.gemm_fm(wg, wgr, KC, A, self.Ares[blk], blk, pg, pgr)
                tf, tfr = self.tmpf.next()
                b.op("act", lambda e, tf=tf, pg=pg: e.activation(out=tf, in_=pg[:, :], func=AF.Sigmoid),
                     reads=[pgr], writes=[tfr])
                of, ofr, ofd = self.of32.next()
                b.op("dve", lambda e, of=of, pa=pa, tf=tf: e.tensor_tensor(out=of, in0=pa[:, :], in1=tf, op=ALU.mult),
                     reads=[par, tfr], writes=[ofr])
                b.dma("sp", ofd, self.cT[j, :, blk * 512:(blk + 1) * 512], of, reads=[ofr],
                      writes=[self.dr("cT", j, blk)])
        import os
        pstop = int(os.environ.get("PROJ_STOP", "9"))
        if pstop <= 1:
            return
        for h in range(10):
            wt, wr = self.wnext()
            dst = self.qT[h] if h < 8 else self.kT[h - 8]
            for blk in range(NB):
                bk, br = self.bank()
                self.gemm_fm(wt, wr, KC, A, self.Ares[blk], blk, bk, br)
                ob, obr, obd = self.obf.next()
                rmode = int(os.environ.get("ROPE_MODE", "2"))
                if rmode == 0:
                    b.op("act", lambda e, ob=ob, bk=bk: e.activation(out=ob, in_=bk[:, :], func=AF.Copy),
                         reads=[br], writes=[obr])
                    b.dma("sp", obd, dst[:, blk * 512:(blk + 1) * 512], ob, reads=[obr],
                          writes=[self.dr("qk", h, blk)])
                    continue
                q32, q32r = self.q32.next()
                b.op("dve", lambda e, q32=q32, bk=bk: e.tensor_copy(out=q32, in_=bk[:, :]), reads=[br], writes=[q32r])
                b.op("act", lambda e, ob=ob, q32=q32: e.activation(out=ob, in_=q32, func=AF.Copy),
                     reads=[q32r], writes=[obr])
                pr, prr = self.bank(6, 8)
                ql, qlr = ql_ring.next()
                b.op("dve", lambda e: e.tensor_tensor(out=ql, in0=q32, in1=ob, op=ALU.subtract),
                     reads=[q32r, obr], writes=[qlr])
                b.group("pe", [lambda e: e.matmul(pr[:, :], lhsT=rmv[:, :], rhs=ob, start=True, stop=False),
                               lambda e: e.matmul(pr[:, :], lhsT=rmv[:, :], rhs=ql, start=False, stop=True)],
                        reads=[obr, qlr, self.k_res], writes=[prr])
                if rmode == 1:
                    b.op("dve", lambda e: e.tensor_copy(out=ob[0:32, :], in_=pr[0:32, :]), reads=[prr], writes=[obr])
                    b.dma("sp", obd, dst[:, blk * 512:(blk + 1) * 512], ob, reads=[obr],
                          writes=[self.dr("qk", h, blk)])
                    continue
                t1, t1r = self.t1.next()
                b.op("dve", lambda e, t1=t1, q32=q32, blk=blk: e.tensor_tensor(
                    out=t1, in0=q32[0:32, :], in1=ropev[:, 0, blk * 512:(blk + 1) * 512], op=ALU.mult),
                    reads=[q32r, rope_r], writes=[t1r])
                t2, t2r = self.t2.next()
                b.op("dve", lambda e, t2=t2, pr=pr, blk=blk: e.tensor_tensor(
                    out=t2, in0=pr[0:32, :], in1=ropev[:, 1, blk * 512:(blk + 1) * 512], op=ALU.mult),
                    reads=[prr, rope_r], writes=[t2r])
                b.op("dve", lambda e, ob=ob, t1=t1, t2=t2: e.tensor_tensor(out=ob[0:32, :], in0=t1, in1=t2, op=ALU.add),
                     reads=[t1r, t2r], writes=[obr])
                b.dma("sp", obd, dst[:, blk * 512:(blk + 1) * 512], ob, reads=[obr],
                      writes=[self.dr("qk", h, blk)])
        if pstop <= 2:
            return
        wv0, wv0r = self.wnext(ahead=2)
        wv1, wv1r = self.wnext(ahead=1)
        for i2 in range(NT // 2):
            bk, br = self.bank()
            fns = []
            for jj in range(2):
                t = 2 * i2 + jj
                for c, wt in enumerate((wv0, wv1)):
                    col = jj * 256 + c * 128
                    for kc in range(KC):
                        fns.append(lambda e, t=t, kc=kc, wt=wt, col=col: e.matmul(
                            bk[:, col:col + 128], lhsT=A[:, kc, t * 128:(t + 1) * 128], rhs=wt[:, kc, :],
                            start=(kc == 0), stop=(kc == KC - 1)))
            b.group("pe", fns, reads=[wv0r, wv1r, self.Ares[i2 // 2]], writes=[br])
            ob, obr, obd = self.obf.next()
            b.op("act", lambda e: e.activation(out=ob, in_=bk[:, :], func=AF.Copy), reads=[br], writes=[obr])
            b.dma("sp", obd, self.V[:, 2 * i2:2 * i2 + 2, :], ob.rearrange("p (a b) -> p a b", a=2),
                  reads=[obr], writes=[self.dr("V", i2)])
        if pstop <= 3:
            return
        for j in range(4):
            wt, wr = self.wnext()
            for blk in range(NB):
                bk, br = self.bank()
                self.gemm_fm(wt, wr, KC, A, self.Ares[blk], blk, bk, br)
                ob, obr, obd = self.obf.next()
                b.op("act", lambda e, ob=ob, bk=bk: e.activation(out=ob, in_=bk[:, :], func=AF.Gelu),
                     reads=[br], writes=[obr])
                b.dma("sp", obd, self.uT[j, :, blk * 512:(blk + 1) * 512], ob, reads=[obr],
                      writes=[self.dr("uT", j, blk)])
        if pstop <= 4:
            return
        for c in range(4):
            wt, wr = self.wnext()
            for i in range(NB):
                bk, br = self.bank()
                fns = []
                for jj in range(4):
                    t = 4 * i + jj
                    for kc in range(KC):
                        fns.append(lambda e, jj=jj, t=t, kc=kc, bk=bk, wt=wt: e.matmul(
                            bk[:, jj * 128:(jj + 1) * 128], lhsT=A[:, kc, t * 128:(t + 1) * 128], rhs=wt[:, kc, :],
                            start=(kc == 0), stop=(kc == KC - 1)))
                b.group("pe", fns, reads=[wr, self.Ares[i]], writes=[br])
                of, ofr, ofd = self.of32.next()
                b.op("act", lambda e, of=of, bk=bk: e.activation(out=of, in_=bk[:, :], func=AF.Gelu),
                     reads=[br], writes=[ofr])
                b.dma("sp", ofd, self.vg[:, 4 * i:4 * i + 4, c * 128:(c + 1) * 128],
                      of.rearrange("p (a b) -> p a b", a=4), reads=[ofr], writes=[self.dr("vg", c, i)])

    def phase_attn(self, l):
        b = self.b
        A = self.A_bf
        scale = 1.0 / float(np.sqrt(128.0))
        qv = self.bview(0, [128, 4, T], BF16)
        kv = self.bview(5120, [128, T], BF16)
        vv = self.bview(6400, [128, NT, 256], BF16)
        ev = [self.bview(8960 + i * 256, [128, 512], BF16) for i in range(6)]
        self.b_phase()
        e_ring = Ring(b, "att_e", ev, mk=self)
        qr, kr, vr = self.newB(), self.newB(), self.newB()
        b.dma("sp", self.mds(), vv, self.V, reads=[self.dr("V", i2) for i2 in range(NT // 2)], writes=[vr])
        for g in range(2):
            b.dma("sp", self.mds(), qv, self.qT[4 * g:4 * g + 4].rearrange("h p t -> p h t"),
                  reads=[self.dr("qk", h, blk) for h in range(4 * g, 4 * g + 4) for blk in range(NB)], writes=[qr])
            b.dma("sp", self.mds(), kv, self.kT[g],
                  reads=[self.dr("qk", 8 + g, blk) for blk in range(NB)], writes=[kr])
            for t in range(NT):
                kts = [kt for kt in (t - 1, t, t + 1) if 0 <= kt < NT]
                qrhs = qv[:, :, t * 128:(t + 1) * 128]
                es = []
                for kt in kts:
                    sb_, sbr = self.bank(0, 6)
                    fns = [lambda e, sb_=sb_, kt=kt, qrhs=qrhs: e.matmul(
                        sb_[:, :], lhsT=kv[:, kt * 128:(kt + 1) * 128], rhs=qrhs, start=True, stop=(kt == t))]
                    if kt != t:
                        mi = 0 if kt < t else 1
                        fns.append(lambda e, sb_=sb_, mi=mi: e.matmul(
                            sb_[:, :], lhsT=self.ident_bf[:, :], rhs=self.mask_bf[:, mi, :], start=False, stop=True))
                    b.group("pe", fns, reads=[qr, kr, self.k_res], writes=[sbr])
                    ee, eer = e_ring.next()
                    b.op("act", lambda e, ee=ee, sb_=sb_: e.activation(out=ee, in_=sb_[:, :], func=AF.Exp, scale=scale),
                         reads=[sbr], writes=[eer])
                    es.append((kt, ee, eer))
                dn, dnr = self.banks[6], self.bres[6]
                ob_, obr_ = self.banks[7], self.bres[7]
                n = len(es)
                b.group("pe", [lambda e, i=i, ee=ee: e.matmul(dn[:, :], lhsT=self.ones_bf[:, :], rhs=ee,
                                                              start=(i == 0), stop=(i == n - 1))
                               for i, (kt, ee, eer) in enumerate(es)],
                        reads=[x[2] for x in es] + [self.k_res], writes=[dnr])
                b.group("pe", [lambda e, i=i, ee=ee, kt=kt: e.matmul(ob_[:, :], lhsT=vv[:, kt, g * 128:(g + 1) * 128], rhs=ee,
                                                                     start=(i == 0), stop=(i == n - 1))
                               for i, (kt, ee, eer) in enumerate(es)],
                        reads=[x[2] for x in es] + [vr], writes=[obr_])
                ds_, dsr = self.small.next()
                for hh in range(4):
                    h = 4 * g + hh
                    b.op("dve", lambda e, hh=hh, h=h, ds_=ds_: e.tensor_scalar(
                        out=ds_[:, hh * 128:(hh + 1) * 128], in0=dn[:, hh * 128:(hh + 1) * 128],
                        scalar1=self.esink[:, h:h + 1], scalar2=None, op0=ALU.add),
                        reads=[dnr, self.pl_res], writes=[dsr])
                b.op("dve", lambda e, ds_=ds_: e.reciprocal(out=ds_, in_=ds_), reads=[dsr], writes=[dsr])
                b.op("dve", lambda e, ds_=ds_, t=t, g=g: e.tensor_tensor(
                    out=A[:, 4 * g:4 * g + 4, t * 128:(t + 1) * 128],
                    in0=ob_[:, :].rearrange("p (a b) -> p a b", a=4),
                    in1=ds_.rearrange("p (a b) -> p a b", a=4), op=ALU.mult),
                    reads=[obr_, dsr], writes=[self.Ares[t // 4]])

    def phase_conv(self, l):
        b = self.b
        A = self.A_bf
        cb = [[self.bview((i * 4 + j) * 544, [128, 544], F32) for j in range(4)] for i in range(2)]
        yb = [[self.bview(4352 + (i * 4 + j) * 512, [128, 512], F32) for j in range(4)] for i in range(2)]
        ysq = [self.bview(8448 + i * 256, [128, 512], BF16) for i in range(2)]
        self.b_phase()
        cres = [[self.newB() for _ in range(4)] for _ in range(2)]
        yres = [[self.newB() for _ in range(4)] for _ in range(2)]
        ysq_ring = Ring(b, "cv_sq", ysq, mk=self)
        yh_ring = Ring(b, "cv_yh", [self.bview(8960, [128, 512], BF16)], mk=self)
        yl_ring = Ring(b, "cv_yl", [self.bview(9216, [128, 512], BF16)], mk=self)
        cds = self.cv_ds
        ci = 0
        wo, _ = self.coff["dww"]
        for blk in range(NB):
            par = blk % 2
            lo = blk * 512 - 15
            hi = blk * 512 + 512 + 15
            for j in range(4):
                c_, cr = cb[par][j], cres[par][j]
                slo, shi = max(lo, 0), min(hi, T)
                if lo < 0:
                    b.op("pool", lambda e, c_=c_: e.memset(c_[:, 0:15], 0.0), writes=[cr])
                if hi > T:
                    b.op("pool", lambda e, c_=c_: e.memset(c_[:, 527:542], 0.0), writes=[cr])
                rd = [self.dr("cT", j, bb) for bb in range(max(blk - 1, 0), min(blk + 2, NB))]
                b.dma("sp", cds[ci % 4], c_[:, slo - lo:shi - lo], self.cT[j, :, slo:shi], reads=rd, writes=[cr])
                ci += 1
            for j in range(4):
                c_, cr = cb[par][j], cres[par][j]
                y_, yr = yb[par][j], yres[par][j]
                w0 = wo + (l * 4 + j) * 31
                bo, _ = self.coff["dwb"]
                b.op("dve", lambda e, y_=y_, c_=c_, w0=w0, j=j: e.tensor_scalar(
                    out=y_, in0=c_[:, 0:512], scalar1=self.CS[:, w0:w0 + 1],
                    scalar2=self.CS[:, bo + l * 4 + j: bo + l * 4 + j + 1], op0=ALU.mult, op1=ALU.add),
                    reads=[cr, self.cs_res], writes=[yr])
                for k in range(1, 31):
                    b.op("dve", lambda e, y_=y_, c_=c_, w0=w0, k=k: e.scalar_tensor_tensor(
                        out=y_, in0=c_[:, k:k + 512], scalar=self.CS[:, w0 + k:w0 + k + 1], in1=y_,
                        op0=ALU.mult, op1=ALU.add), reads=[cr, yr, self.cs_res], writes=[yr])
            sm, smr = self.banks[4], self.bres[4]
            s2, s2r = self.banks[5], self.bres[5]
            for j in range(4):
                yh, yhr = yh_ring.next()
                yl, ylr = yl_ring.next()
                b.op("dve", lambda e: e.tensor_copy(out=yh, in_=yb[par][j]), reads=[yres[par][j]], writes=[yhr])
                b.op("dve", lambda e: e.tensor_tensor(out=yl, in0=yb[par][j], in1=yh, op=ALU.subtract),
                     reads=[yres[par][j], yhr], writes=[ylr])
                b.group("pe", [lambda e: e.matmul(sm[:, :], lhsT=self.ones_bf[:, :], rhs=yh, start=(j == 0), stop=False),
                               lambda e: e.matmul(sm[:, :], lhsT=self.ones_bf[:, :], rhs=yl, start=False, stop=(j == 3))],
                        reads=[yhr, ylr, self.k_res], writes=[smr])
                sq, sqr = ysq_ring.next()
                b.op("act", lambda e: e.activation(out=sq, in_=yb[par][j], func=AF.Square),
                     reads=[yres[par][j]], writes=[sqr])
                b.group("pe", [lambda e: e.matmul(s2[:, :], lhsT=self.ones_bf[:, :], rhs=sq,
                                                  start=(j == 0), stop=(j == 3))],
                        reads=[sqr, self.k_res], writes=[s2r])
            mean, mr = self.small.next()
            rstd, rr = self.small.next()
            b.op("dve", lambda e, mean=mean: e.tensor_scalar(out=mean, in0=sm[:, :], scalar1=1.0 / 512, scalar2=None,
                                                             op0=ALU.mult), reads=[smr], writes=[mr])
            tf, tfr = self.tmpf.next()
            b.op("dve", lambda e, tf=tf, mean=mean: e.tensor_tensor(out=tf, in0=mean, in1=mean, op=ALU.mult),
                 reads=[mr], writes=[tfr])
            b.op("dve", lambda e, rstd=rstd, tf=tf: e.scalar_tensor_tensor(
                out=rstd, in0=s2[:, :], scalar=1.0 / 512, in1=tf, op0=ALU.mult, op1=ALU.subtract),
                reads=[s2r, tfr], writes=[rr])
            self.rsqrt_eps(rstd, rr)
            go, _ = self.coff["clg"]
            bo2, _ = self.coff["clb"]
            for j in range(4):
                y_, yr = yb[par][j], yres[par][j]
                tf, tfr = self.tmpf.next()
                b.op("dve", lambda e, tf=tf, y_=y_, mean=mean: e.tensor_tensor(out=tf, in0=y_, in1=mean, op=ALU.subtract),
                     reads=[yr, mr], writes=[tfr])
                b.op("dve", lambda e, tf=tf, rstd=rstd: e.tensor_tensor(out=tf, in0=tf, in1=rstd, op=ALU.mult),
                     reads=[tfr, rr], writes=[tfr])
                b.op("dve", lambda e, tf=tf, j=j: e.tensor_scalar(
                    out=tf, in0=tf, scalar1=self.CS[:, go + l * 4 + j: go + l * 4 + j + 1],
                    scalar2=self.CS[:, bo2 + l * 4 + j: bo2 + l * 4 + j + 1], op0=ALU.mult, op1=ALU.add),
                    reads=[tfr, self.cs_res], writes=[tfr])
                b.op("act", lambda e, tf=tf, j=j, blk=blk: e.activation(
                    out=A[:, 8 + j, blk * 512:(blk + 1) * 512], in_=tf, func=AF.Silu),
                    reads=[tfr], writes=[self.Ares[blk]])

    def phase_sgu(self, l):
        b = self.b
        A = self.A_bf
        ub = self.bview(0, [128, 4, T], BF16)
        vgb = [self.bview(5120 + i * 2048, [128, 4, 512], F32) for i in range(2)]
        vnb = [self.bview(9216 + i * 1024, [128, 4, 512], BF16) for i in range(2)]
        st6 = self.bview(11264, [128, 8], F32)
        mv = self.bview(11272, [128, 8], F32)
        self.b_phase()
        vg_ring = Ring(b, f"sg_v{l}", vgb, ndsem=self.ds_p, mk=self)
        vn_ring = Ring(b, "sg_n", vnb, mk=self)
        ur = self.newB()
        sres = self.newB()
        b.dma("sp", self.mds(), ub, self.uT.rearrange("j p t -> p j t"),
              reads=[self.dr("uT", j, blk) for j in range(4) for blk in range(NB)], writes=[ur])
        lng = self.PL[:, 0:512]
        lnb = self.PL[:, 512:1024]
        bsb = self.PL[:, 1024:1536].rearrange("p (a b) -> p a b", a=4)
        for i in range(NB):
            vgt, vgr, vgd = vg_ring.next()
            b.dma("sp", vgd, vgt, self.vg[:, 4 * i:4 * i + 4, :],
                  reads=[self.dr("vg", c, i) for c in range(4)], writes=[vgr])
            vn, vnr = vn_ring.next()
            for jj in range(4):
                b.op("dve", lambda e, vgt=vgt, jj=jj: e.bn_stats(out=st6[:, 0:6], in_=vgt[:, jj, :]),
                     reads=[vgr], writes=[sres])
                b.op("dve", lambda e: e.bn_aggr(out=mv[:, 0:2], in_=st6[:, 0:6]), reads=[sres], writes=[sres])
                self.rsqrt_eps(mv[:, 1:2], sres)
                b.op("dve", lambda e, vgt=vgt, jj=jj: e.tensor_scalar(
                    out=vgt[:, jj, :], in0=vgt[:, jj, :], scalar1=mv[:, 0:1], scalar2=mv[:, 1:2],
                    op0=ALU.subtract, op1=ALU.mult), reads=[vgr, sres], writes=[vgr])
                b.op("dve", lambda e, vgt=vgt, jj=jj: e.tensor_tensor(out=vgt[:, jj, :], in0=vgt[:, jj, :], in1=lng, op=ALU.mult),
                     reads=[vgr, self.pl_res], writes=[vgr])
                b.op("dve", lambda e, vgt=vgt, vn=vn, jj=jj: e.tensor_tensor(out=vn[:, jj, :], in0=vgt[:, jj, :], in1=lnb, op=ALU.add),
                     reads=[vgr, self.pl_res], writes=[vnr])
            for h in range(4):
                bk, br = self.bank(0, 4)
                b.group("pe", [lambda e, jj=jj, h=h, bk=bk, vn=vn: e.matmul(
                    bk[:, jj * 128:(jj + 1) * 128], lhsT=vn[:, jj, h * 128:(h + 1) * 128], rhs=self.wsT_bf[:, h, :],
                    start=True, stop=True) for jj in range(4)], reads=[vnr, self.pl_res], writes=[br])
                tf, tfr = self.tmpf.next()
                b.op("dve", lambda e, tf=tf, bk=bk, h=h: e.tensor_tensor(
                    out=tf.rearrange("p (a b) -> p a b", a=4), in0=bk[:, :].rearrange("p (a b) -> p a b", a=4),
                    in1=bsb[:, h:h + 1, :].to_broadcast([128, 4, 128]), op=ALU.add),
                    reads=[br, self.pl_res], writes=[tfr])
                b.op("dve", lambda e, tf=tf, h=h, i=i: e.tensor_tensor(
                    out=A[:, 12 + h, i * 512:(i + 1) * 512], in0=tf, in1=ub[:, h, i * 512:(i + 1) * 512], op=ALU.mult),
                    reads=[tfr, ur], writes=[self.Ares[i]])

    def residual_prefetch(self, j, blk):
        xi, xir, xid = self.xin.next()
        sl = self.xT[2 * blk:2 * blk + 2, :, j, :].rearrange("h p t -> p h t")
        xrs = [self.xres[j][2 * blk], self.xres[j][2 * blk + 1]]
        self.b.dma("sp", xid, xi.rearrange("p (h t) -> p h t", h=2), sl, reads=xrs, writes=[xir])
        return xi, xir, sl, xrs

    def residual_epilogue(self, bk, br, pre):
        b = self.b
        xi, xir, sl, xrs = pre
        xo, xor_, xod = self.xout.next()
        b.op("dve", lambda e, xo=xo, xi=xi, bk=bk: e.tensor_tensor(out=xo, in0=bk[:, :], in1=xi, op=ALU.add),
             reads=[br, xir], writes=[xor_])
        b.dma("sp", xod, sl, xo.rearrange("p (h t) -> p h t", h=2), reads=[xor_], writes=xrs)

    def residual_gemm_loop(self, kc_n, act, actres):
        LA = 3
        items = [(j, blk) for j in range(KC) for blk in range(NB)]
        pre = {}
        for i in range(min(LA, len(items))):
            pre[i] = self.residual_prefetch(*items[i])
        wt = wr = None
        for i, (j, blk) in enumerate(items):
            if blk == 0:
                wt, wr = self.wnext()
            if i + LA < len(items):
                pre[i + LA] = self.residual_prefetch(*items[i + LA])
            bk, br = self.bank()
            self.gemm_fm(wt, wr, kc_n, act, actres[blk], blk, bk, br)
            self.residual_epilogue(bk, br, pre.pop(i))

    def phase_wout(self, l):
        self.residual_gemm_loop(KC, self.A_bf, self.Ares)

    def phase_ffn(self, l):
        b = self.b
        act = self.bview(0, [128, FPP, T], BF16)
        self.b_phase()
        actres = [self.newB() for _ in range(NB)]
        for p in range(NPASS):
            for fi in range(FPP):
                wg, wgr = self.wnext(ahead=2)
                wu, wur = self.wnext(ahead=1)
                for blk in range(NB):
                    pg, pgr = self.bank()
                    self.gemm_fm(wg, wgr, KC, self.A_bf, self.Ares[blk], blk, pg, pgr)
                    pu, pur = self.bank()
                    self.gemm_fm(wu, wur, KC, self.A_bf, self.Ares[blk], blk, pu, pur)
                    tf, tfr = self.tmpf.next()
                    b.op("act", lambda e, tf=tf, pg=pg: e.activation(out=tf, in_=pg[:, :], func=AF.Silu),
                         reads=[pgr], writes=[tfr])
                    b.op("dve", lambda e, tf=tf, pu=pu, fi=fi, blk=blk: e.tensor_tensor(
                        out=act[:, fi, blk * 512:(blk + 1) * 512], in0=pu[:, :], in1=tf, op=ALU.mult),
                        reads=[pur, tfr], writes=[actres[blk]])
            self.residual_gemm_loop(FPP, act, actres)

    def phase_final(self):
        b = self.b
        go, _ = self.coff["gf"]
        gv = self.CS[:, go:go + 16]
        ident = self.cview("ident")
        self.b_phase()
        xs_ring = Ring(b, "fin_x", [self.bview(i * 4096, [128, KC, 256], F32) for i in range(2)], ndsem=self.ds_p, mk=self)
        sq_ring = Ring(b, "fin_q", [self.bview(8192 + i * 2048, [128, KC, 256], BF16) for i in range(1)], mk=self)
        rs_ring = Ring(b, "fin_r", [self.bview(10240 + i * 256, [128, 256], F32) for i in range(1)], mk=self)
        ot_views = [self.A[:, i * 2048:(i + 1) * 2048] for i in range(2)]
        ot_ring = Ring(b, "fin_o", ot_views, ndsem=self.ds_q)
        a_all = list(self.Ares)
        self.final_evs = []
        for hb in range(T // 256):
            blk = hb // 2
            t0 = hb * 256
            xs, xr, xd = xs_ring.next()
            b.dma("sp", xd, xs, self.xT[hb], reads=[self.xres[kc][hb] for kc in range(KC)], writes=[xr])
            sq, sqr = sq_ring.next()
            b.op("act", lambda e, sq=sq, xs=xs: e.activation(out=sq, in_=xs, func=AF.Square), reads=[xr], writes=[sqr])
            bk, br = self.bank(4, 6)
            fns = [lambda e, kc=kc, bk=bk, sq=sq: e.matmul(bk[:, 0:256], lhsT=self.ones_bf[:, :], rhs=sq[:, kc, :],
                                                          start=(kc == 0), stop=(kc == KC - 1)) for kc in range(KC)]
            b.group("pe", fns, reads=[sqr, self.k_res], writes=[br])
            rs, rsr = rs_ring.next()
            b.op("dve", lambda e, rs=rs, bk=bk: e.tensor_scalar(out=rs, in0=bk[:, 0:256], scalar1=1.0 / D, scalar2=None,
                                                               op0=ALU.mult), reads=[br], writes=[rsr])
            self.rsqrt_eps(rs, rsr)
            for kc in range(KC):
                b.op("dve", lambda e, kc=kc, xs=xs, rs=rs: e.scalar_tensor_tensor(
                    out=xs[:, kc, :], in0=xs[:, kc, :], scalar=gv[:, kc:kc + 1], in1=rs,
                    op0=ALU.mult, op1=ALU.mult), reads=[xr, rsr, self.cs_res], writes=[xr])
            for tt in range(2):
                ot, otr, otd = ot_ring.next()
                for g in range(4):
                    pb, pbr = self.bank(0, 4)
                    fns = [lambda e, j=j, g=g, pb=pb, xs=xs, tt=tt: e.transpose(
                        out=pb[:, j * 128:(j + 1) * 128], in_=xs[:, 4 * g + j, tt * 128:(tt + 1) * 128],
                        identity=ident) for j in range(4)]
                    b.group("pe", fns, reads=[xr, self.cs_res], writes=[pbr])
                    if g % 2 == 0:
                        b.op("dve", lambda e, ot=ot, pb=pb, g=g: e.tensor_copy(out=ot[:, g * 512:(g + 1) * 512], in_=pb[:, :]),
                             reads=[pbr], writes=[otr] + a_all)
                    else:
                        b.op("act", lambda e, ot=ot, pb=pb, g=g: e.activation(out=ot[:, g * 512:(g + 1) * 512], in_=pb[:, :], func=AF.Copy),
                             reads=[pbr], writes=[otr] + a_all)
                    a_all = []
                tok0 = t0 + tt * 128
                ev = b.dma("sp", otd, self.out[tok0:tok0 + 128, :], ot, reads=[otr], writes=[Res()])
                self.final_evs.append(ev)

    def finish(self):
        b = self.b
        for d in b.all_ds:
            if d.count:
                b._wait("sp", (d.key, d.count))
        b.streams["sp"].append(("ins", lambda e: e.nop(), None, 0))

    def weight_tiles(self):
        tiles = []
        for l in range(self.L):
            for i in range(NWIN):
                tiles.append((self.w_in[l, i], KC))
            for i in range(KC):
                tiles.append((self.w_out[l, i], KC))
            for p in range(NPASS):
                for fi in range(FPP):
                    f = p * FPP + fi
                    tiles.append((self.w_gate[l, f], KC))
                    tiles.append((self.w_up[l, f], KC))
                for j in range(KC):
                    tiles.append((self.w_down[l, p, j], FPP))
        return tiles

    def build(self, upto=None):
        order = ["in", "norm1", "proj", "attn", "conv", "sgu", "wout", "norm2", "ffn"]
        stop = len(order) if upto is None else order.index(upto) + 1
        with contextlib.ExitStack() as st:
            self.setup(st)
            self.load_consts()
            self.wstream_init(self.weight_tiles())
            if stop >= 1:
                self.phase_in()
            for l in range(self.L):
                self.load_layer_consts(l)
                steps = [None, lambda: self.phase_norm("g1", l), lambda: self.phase_proj(l), lambda: self.phase_attn(l),
                         lambda: self.phase_conv(l), lambda: self.phase_sgu(l), lambda: self.phase_wout(l),
                         lambda: self.phase_norm("g2", l), lambda: self.phase_ffn(l)]
                for i in range(1, min(stop, len(order))):
                    if self.debug and l == 0 and order[i] == "wout":
                        self.b.dma("sp", self.mds(), self.dbgA, self.A_bf, reads=self.Ares, writes=[Res()])
                    steps[i]()
            if upto is None:
                self.phase_final()
            self.finish()
            self.b.replay()
        return self.nc


def _win_perm():
    cols = []
    for j in range(4):
        cols += list(range(1536 + j * 128, 1536 + (j + 1) * 128))
        cols += list(range(2048 + j * 128, 2048 + (j + 1) * 128))
    cols += list(range(0, 1024))
    cols += list(range(1024, 1280))
    cols += list(range(1280, 1536))
    cols += list(range(2560, 3072))
    cols += list(range(3072, 3584))
    return np.array(cols)


def _tile_kc(w):
    L, K, N = w.shape
    return np.ascontiguousarray(w.reshape(L, K // 128, 128, N // 128, 128).transpose(0, 3, 2, 1, 4))


def prep_weights(w_in, w_out, w_gate, w_up, w_down):
    L = w_in.shape[0]
    d = {}
    d["w_in"] = _tile_kc(np.asarray(w_in)[:, :, _win_perm()])
    d["w_out"] = _tile_kc(np.asarray(w_out))
    d["w_gate"] = _tile_kc(np.asarray(w_gate))
    d["w_up"] = _tile_kc(np.asarray(w_up))
    wd = np.asarray(w_down).reshape(L, NPASS, FPP, 128, KC, 128)
    d["w_down"] = np.ascontiguousarray(wd.transpose(0, 1, 4, 3, 2, 5))
    return d


def prep_consts(L, mix_norm_g, ffn_norm_g, final_norm_g, conv_dw_w, conv_dw_b, conv_ln_g, conv_ln_b,
                sink, sgu_ln_g, sgu_ln_b, sgu_w, sgu_b):
    off, words = _const_layout(L)
    cst = np.zeros((128, words), np.float32)

    def put(name, arr):
        o, n = off[name]
        assert arr.shape == (128, n), (name, arr.shape, n)
        cst[:, o:o + n] = arr

    f = np.float32
    put("g1", np.asarray(mix_norm_g, f).reshape(L, 16, 128).transpose(2, 0, 1).reshape(128, L * 16))
    put("g2", np.asarray(ffn_norm_g, f).reshape(L, 16, 128).transpose(2, 0, 1).reshape(128, L * 16))
    put("gf", np.asarray(final_norm_g, f).reshape(16, 128).T)
    put("dww", np.asarray(conv_dw_w, f).reshape(L, 31, 4, 128).transpose(3, 0, 2, 1).reshape(128, L * 4 * 31))
    put("dwb", np.asarray(conv_dw_b, f).reshape(L, 4, 128).transpose(2, 0, 1).reshape(128, L * 4))
    put("clg", np.asarray(conv_ln_g, f).reshape(L, 4, 128).transpose(2, 0, 1).reshape(128, L * 4))
    put("clb", np.asarray(conv_ln_b, f).reshape(L, 4, 128).transpose(2, 0, 1).reshape(128, L * 4))
    put("sink", np.broadcast_to(np.asarray(sink, f).reshape(1, L * 8), (128, L * 8)))
    put("ident", np.eye(128, dtype=f))
    rm = np.zeros((128, 32), f)
    for m in range(32):
        rm[(m + 16) % 32, m] = 1.0
    put("rm", rm)
    kk = np.arange(128)[:, None]
    qq = np.arange(128)[None, :]
    mL = np.where(kk >= qq, 0.0, NEG).astype(f)
    mR = np.where(kk <= qq, 0.0, NEG).astype(f)
    msk = np.stack([np.tile(mL, (1, 4)), np.tile(mR, (1, 4))], axis=1).astype(f)
    plc = np.zeros((L, 128, PL_WORDS), f)
    plc[:, :, 0:512] = np.asarray(sgu_ln_g, f)[:, None, :]
    plc[:, :, 512:1024] = np.asarray(sgu_ln_b, f)[:, None, :]
    plc[:, :, 1024:1536] = np.asarray(sgu_b, f).reshape(L, 1, 512)
    plc[:, :, 1536:2048] = np.asarray(sgu_w, f).transpose(0, 3, 1, 2).reshape(L, 128, 512)
    return cst, plc, msk


def rope_table(s0):
    pos = np.arange(s0, s0 + T, dtype=np.float32)
    inv = (np.float32(500000.0) ** (-np.arange(0, 32, 2, dtype=np.float32) / np.float32(32))).astype(np.float32)
    ang = (pos[:, None] * inv[None, :]).astype(np.float32)
    c = np.cos(ang).astype(np.float32).T
    s = np.sin(ang).astype(np.float32).T
    tab = np.zeros((32, 2, T), np.float32)
    tab[0:16, 0] = c
    tab[16:32, 0] = c
    tab[0:16, 1] = -s
    tab[16:32, 1] = s
    return tab


_PROG = {}


def get_prog(L, debug=False):
    key = (L, debug)
    if key not in _PROG:
        _PROG[key] = MK(L, debug=debug).build()
    return _PROG[key]


def core_starts():
    return [(c // 2, 0 if c % 2 == 0 else SEQ - T) for c in range(8)]


def kernel(x, mix_norm_g, w_in, sink, conv_dw_w, conv_dw_b, conv_ln_g, conv_ln_b,
           sgu_ln_g, sgu_ln_b, sgu_w, sgu_b, w_out, ffn_norm_g, w_gate, w_up, w_down,
           final_norm_g):
    x = np.asarray(x, np.float32)
    L = DEPTH
    wd = prep_weights(np.asarray(w_in, np.float32), np.asarray(w_out, np.float32), np.asarray(w_gate, np.float32),
                      np.asarray(w_up, np.float32), np.asarray(w_down, np.float32))
    cst, plc, msk = prep_consts(L, mix_norm_g, ffn_norm_g, final_norm_g, conv_dw_w, conv_dw_b, conv_ln_g, conv_ln_b,
                           sink, sgu_ln_g, sgu_ln_b, sgu_w, sgu_b)
    nc = get_prog(L)
    in_maps = []
    for (bi, s0) in core_starts():
        m = dict(wd)
        m["x_in"] = np.ascontiguousarray(x[bi, s0:s0 + T, :])
        m["cst"] = cst
        m["plc"] = plc
        m["msk"] = msk
        m["rope"] = rope_table(s0)
        in_maps.append(m)
    res = run_bass_kernel_spmd(nc, in_maps, core_ids=list(range(8)))
    out = np.empty((BATCH, SEQ, D), np.float32)
    for c, (bi, s0) in enumerate(core_starts()):
        o = res.results[c]["out"]
        if s0 == 0:
            out[bi, 0:OWN] = o[0:OWN]
        else:
            out[bi, SEQ - OWN:SEQ] = o[T - OWN:T]
    return out
```

```python
import contextlib
import os
import types
from math import prod

import numpy as np
import concourse.bass as bass
import concourse.mybir as mybir
from concourse.bass_utils import run_bass_kernel_spmd

F32 = mybir.dt.float32
BF16 = mybir.dt.bfloat16
AF = mybir.ActivationFunctionType
ALU = mybir.AluOpType

D = 2048
SEQ = 4096
BATCH = 4
DEPTH = 4
T = 2560
OWN = 2048
NT = T // 128
NB = T // 512
KC = D // 128
DFF = 5632
NF = DFF // 128
NPASS = 4
FPP = NF // NPASS
NWIN = 28
EPS = 1e-6
NEG = -30000.0

ENGS = ["pe", "act", "dve", "pool", "sp"]
SAME_ENGINE_SYNC = True


def _freeze(fn):
    if fn.__closure__ is None:
        return fn
    cells = []
    for c in fn.__closure__:
        try:
            cells.append(types.CellType(c.cell_contents))
        except ValueError:
            cells.append(c)
    g = types.FunctionType(fn.__code__, fn.__globals__, fn.__name__, fn.__defaults__, tuple(cells))
    g.__kwdefaults__ = fn.__kwdefaults__
    return g


class Res:
    __slots__ = ("w", "r")

    def __init__(self):
        self.w = None
        self.r = []


class DSem:
    __slots__ = ("key", "count")

    def __init__(self, key):
        self.key = key
        self.count = 0


class Builder:
    def __init__(self, nc, stack):
        self.nc = nc
        self.stack = stack
        self.streams = {e: [] for e in ENGS}
        self.sems = []
        self.cnt = {}
        self.waited = {e: {} for e in ENGS}
        self.esem = {}
        for e in ENGS:
            self.esem[e] = self.new_sem("e_" + e)
            self.cnt[e] = 0
        self.n_ins = 0

    def new_sem(self, name):
        h = self.stack.enter_context(self.nc.semaphore(name))
        self.sems.append(h)
        return len(self.sems) - 1

    def dsem_pool(self, name, n):
        ds = [DSem(self.new_sem(f"{name}{i}")) for i in range(n)]
        self.all_ds = getattr(self, "all_ds", []) + ds
        return ds

    def sb(self, name, shape, dtype):
        return self.stack.enter_context(self.nc.sbuf_tensor(name, list(shape), dtype))[:]

    def ps(self, name, shape, dtype):
        return self.stack.enter_context(self.nc.psum_tensor(name, list(shape), dtype))[:]

    def _wait(self, eng, ev):
        if ev is None:
            return
        s, v = ev
        if s == self.esem[eng] and (eng == "pe" or not SAME_ENGINE_SYNC):
            return
        if self.waited[eng].get(s, 0) >= v:
            return
        self.waited[eng][s] = v
        self.streams[eng].append(("wait", s, v))

    def _deps(self, eng, reads, writes):
        for r in reads:
            self._wait(eng, r.w)
        for w in writes:
            self._wait(eng, w.w)
            for ev in w.r:
                self._wait(eng, ev)

    def _mark(self, ev, reads, writes):
        for r in reads:
            r.r = [e for e in r.r if e[0] != ev[0]]
            r.r.append(ev)
        for w in writes:
            w.w = ev
            w.r = []

    def op(self, eng, fn, reads=(), writes=()):
        self._deps(eng, reads, writes)
        self.cnt[eng] += 1
        ev = (self.esem[eng], self.cnt[eng])
        self.streams[eng].append(("ins", _freeze(fn), ev[0], 1))
        self._mark(ev, reads, writes)
        self.n_ins += 1
        return ev

    def group(self, eng, fns, reads=(), writes=()):
        self._deps(eng, reads, writes)
        for fn in fns[:-1]:
            self.streams[eng].append(("ins", _freeze(fn), None, 0))
        self.cnt[eng] += 1
        ev = (self.esem[eng], self.cnt[eng])
        self.streams[eng].append(("ins", _freeze(fns[-1]), ev[0], 1))
        self._mark(ev, reads, writes)
        self.n_ins += len(fns)
        return ev

    def dma(self, q, dsem, out, in_, reads=(), writes=()):
        self._deps(q, reads, writes)
        if dsem.count:
            self._wait(q, (dsem.key, dsem.count))
        dsem.count += 16
        ev = (dsem.key, dsem.count)
        self.streams[q].append(("ins", lambda e, o=out, i=in_: e.dma_start(out=o, in_=i), ev[0], 16))
        self._mark(ev, reads, writes)
        self.n_ins += 1
        return ev

    def replay(self):
        nc = self.nc
        sems = self.sems
        streams = self.streams

        def run(e, st):
            for it in st:
                if it[0] == "wait":
                    e.wait_ge(sems[it[1]], it[2])
                else:
                    ins = it[1](e)
                    if it[2] is not None:
                        ins.then_inc(sems[it[2]], it[3])

        with nc.Block() as block:
            @block.tensor
            def _(e):
                run(e, streams["pe"])

            @block.scalar
            def _(e):
                run(e, streams["act"])

            @block.vector
            def _(e):
                run(e, streams["dve"])

            @block.gpsimd
            def _(e):
                run(e, streams["pool"])

            @block.sync
            def _(e):
                run(e, streams["sp"])


class Ring:
    def __init__(self, b, name, views, ndsem=0, mk=None):
        self.views = views
        self.res = [(mk.newB() if mk is not None else Res()) for _ in views]
        if ndsem and not isinstance(ndsem, int):
            self.ds = ndsem
            assert len(self.ds) >= len(views)
        else:
            self.ds = b.dsem_pool(name, len(views)) if ndsem else None
        self.i = 0

    def next(self):
        k = self.i % len(self.views)
        self.i += 1
        if self.ds:
            return self.views[k], self.res[k], self.ds[k]
        return self.views[k], self.res[k]


def _const_layout(L):
    off = {}
    o = 0
    for name, n in [("g1", L * 16), ("g2", L * 16), ("gf", 16), ("dww", L * 4 * 31),
                    ("dwb", L * 4), ("clg", L * 4), ("clb", L * 4), ("sink", L * 8),
                    ("ident", 128), ("rm", 32)]:
        off[name] = (o, n)
        o += n
    return off, o


PL_WORDS = 512 + 512 + 512 + 512


class MK:
    def __init__(self, L, debug=False, l0=0):
        self.L = L
        self.debug = debug
        nc = self.nc = bass.Bass("TRN2", target_bir_lowering=False)
        dk = "ExternalOutput" if debug else None

        def dram(name, shape, dt, kind=None):
            if kind is None:
                return nc.dram_tensor(name, list(shape), dt).ap()
            return nc.dram_tensor(name, list(shape), dt, kind=kind).ap()

        self.coff, self.cwords = _const_layout(L)
        self.x_in = dram("x_in", [T, D], F32, "ExternalInput")
        self.w_in = dram("w_in", [L, NWIN, 128, KC, 128], F32, "ExternalInput")
        self.w_out = dram("w_out", [L, KC, 128, KC, 128], F32, "ExternalInput")
        self.w_gate = dram("w_gate", [L, NF, 128, KC, 128], F32, "ExternalInput")
        self.w_up = dram("w_up", [L, NF, 128, KC, 128], F32, "ExternalInput")
        self.w_down = dram("w_down", [L, NPASS, KC, 128, FPP, 128], F32, "ExternalInput")
        self.cst = dram("cst", [128, self.cwords], F32, "ExternalInput")
        self.plc = dram("plc", [L, 128, PL_WORDS], F32, "ExternalInput")
        self.rope = dram("rope", [32, 2, T], F32, "ExternalInput")
        self.msk = dram("msk", [128, 2, 512], F32, "ExternalInput")
        self.out = dram("out", [OWN, D], F32, "ExternalOutput")
        self.xT = dram("xT", [T // 256, 128, KC, 256], F32, dk)
        self.qT = dram("qT", [8, 128, T], BF16, dk)
        self.kT = dram("kT", [2, 128, T], BF16, dk)
        self.V = dram("V", [128, NT, 256], BF16, dk)
        self.cT = dram("cT", [4, 128, T], BF16, dk)
        self.uT = dram("uT", [4, 128, T], BF16, dk)
        self.vg = dram("vg", [128, NT, 512], F32, dk)
        if debug:
            self.dbgA = dram("dbgA", [128, KC, T], BF16, dk)

    def setup(self, st):
        nc = self.nc
        b = self.b = Builder(nc, st)
        L = self.L
        A_W = KC * T // 2
        B_W = 14080
        WS_W = 2 * 2048
        WB_W = 3 * 1024
        self.A = b.sb("A", [128, A_W], F32)
        self.Bm = b.sb("Bm", [128, B_W], F32)
        self.WS = b.sb("WS", [128, WS_W], F32)
        self.WB = b.sb("WB", [128, WB_W], F32)
        self.CS = b.sb("CS", [128, self.cwords], F32)
        self.PL = b.sb("PL", [128, PL_WORDS], F32)
        self.wsT_bf = b.sb("wsT_bf", [128, 4, 128], BF16)
        self.mask_bf = b.sb("mask_bf", [128, 2, 512], BF16)
        self.ones_bf = b.sb("ones_bf", [128, 128], BF16)
        self.rm_bf = b.sb("rm_bf", [128, 128], BF16)
        self.ident_bf = b.sb("ident_bf", [128, 128], BF16)
        self.esink = b.sb("esink", [128, 8], F32)
        self.A_bf = self.A[:, :].bitcast(BF16).rearrange("p (a b) -> p a b", a=KC)
        self.Ares = [Res() for _ in range(NB)]
        self.tmpf = Ring(b, "tmpf", [b.sb(f"tmpf{i}", [128, 512], F32) for i in range(2)])
        self.obf = Ring(b, "obf", [b.sb(f"obf{i}", [128, 512], BF16) for i in range(3)], ndsem=1)
        self.xin = Ring(b, "xin", [b.sb(f"xin{i}", [128, 512], F32) for i in range(4)], ndsem=1)
        self.xout = Ring(b, "xout", [b.sb(f"xout{i}", [128, 512], F32) for i in range(3)], ndsem=1)
        self.of32 = self.xout
        self.small = Ring(b, "small", [b.sb(f"small{i}", [128, 512], F32) for i in range(2)])
        self.banks = [b.ps(f"bank{i}", [128, 512], F32) for i in range(8)]
        self.bres = [Res() for _ in range(8)]
        self.bank_i = 0
        ws3 = self.WS[:, :].rearrange("p (s k c) -> p s k c", s=2, k=KC)
        wb3 = self.WB[:, :].bitcast(BF16).rearrange("p (s k c) -> p s k c", s=3, k=KC)
        self.wstage = [(ws3[:, i], Res(), d) for i, d in enumerate(b.dsem_pool("wld", 2))]
        self.wbf = [(wb3[:, i], Res()) for i in range(3)]
        self.misc_ds = b.dsem_pool("misc", 4)
        self.misc_i = 0
        self.xres = [[Res() for _ in range(T // 256)] for _ in range(KC)]
        self.dres = {}
        self.cs_res = Res()
        self.pl_res = Res()
        self.b_fence = []
        self.b_res = []
        self.cv_ds = b.dsem_pool("cvl", 4)
        self.ds_p = b.dsem_pool("dsp", 2)
        self.ds_q = b.dsem_pool("dsq", 2)

    def newB(self):
        r = Res()
        r.r = list(self.b_fence)
        self.b_res.append(r)
        return r

    def b_phase(self):
        best = {}
        for ev in self.b_fence:
            best[ev[0]] = max(best.get(ev[0], 0), ev[1])
        for r in self.b_res:
            for ev in ([r.w] if r.w else []) + r.r:
                best[ev[0]] = max(best.get(ev[0], 0), ev[1])
        self.b_fence = [(k, v) for k, v in best.items()]
        self.b_res = []

    def ntm(self, l):
        return OWN // 128 + (self.L - 1 - l)

    @staticmethod
    def bw(blk, ntiles):
        return max(0, min(512, ntiles * 128 - blk * 512))

    def nbm(self, l):
        return (OWN // 128 + (self.L - 1 - l) + 3) // 4

    def dr(self, *key):
        r = self.dres.get(key)
        if r is None:
            r = self.dres[key] = Res()
        return r

    def mds(self):
        d = self.misc_ds[self.misc_i % len(self.misc_ds)]
        self.misc_i += 1
        return d

    def bank(self, lo=0, hi=4):
        n = hi - lo
        k = lo + (self.bank_i % n)
        self.bank_i += 1
        return self.banks[k], self.bres[k]

    def cview(self, name):
        o, n = self.coff[name]
        return self.CS[:, o:o + n]

    def bview(self, off, shape, dtype):
        n = prod(shape[1:])
        nw = n // 2 if dtype == BF16 else n
        assert off + nw <= self.Bm.shape[1], (off, nw)
        ap = self.Bm[:, off:off + nw]
        if dtype == BF16:
            ap = ap.bitcast(BF16)
        if len(shape) == 3:
            ap = ap.rearrange("p (a b) -> p a b", a=shape[1])
        return ap

    def wstream_init(self, tiles):
        self.wtiles = tiles
        self.w_cast = 0
        self.w_used = 0
        self.w_loaded = 0
        for _ in range(2):
            self._wload()

    def _wload(self):
        n = self.w_loaded
        if n >= len(self.wtiles):
            return
        ap, kc = self.wtiles[n]
        sv, sr, ds = self.wstage[n % 2]
        self.b.dma("sp", ds, sv[:, 0:kc, :], ap, writes=[sr])
        self.w_loaded += 1

    def _wcast(self):
        n = self.w_cast
        if n >= len(self.wtiles):
            return
        _, kc = self.wtiles[n]
        sv, sr, _ = self.wstage[n % 2]
        bv, br = self.wbf[n % 3]
        self.b.op("pool", lambda e, o=bv[:, 0:kc, :], i=sv[:, 0:kc, :]: e.tensor_copy(out=o, in_=i),
                  reads=[sr], writes=[br])
        self.w_cast += 1
        self._wload()

    def wnext(self, ahead=1):
        n = self.w_used
        while self.w_cast <= min(n + ahead, len(self.wtiles) - 1):
            self._wcast()
        self.w_used += 1
        return self.wbf[n % 3]

    def rsqrt_eps(self, ap, res):
        b = self.b
        b.op("dve", lambda e: e.tensor_scalar(out=ap, in0=ap, scalar1=EPS, scalar2=None, op0=ALU.add),
             reads=[res], writes=[res])
        b.op("act", lambda e: e.activation(out=ap, in_=ap, func=AF.Sqrt), reads=[res], writes=[res])
        b.op("dve", lambda e: e.reciprocal(out=ap, in_=ap), reads=[res], writes=[res])

    def gemm_fm(self, wt, wr, kc_n, act, ares, blk, bank, bres, n=512):
        fns = []
        for kc in range(kc_n):
            fns.append(lambda e, kc=kc: e.matmul(bank[:, 0:n], lhsT=wt[:, kc, :],
                                                 rhs=act[:, kc, blk * 512:blk * 512 + n],
                                                 start=(kc == 0), stop=(kc == kc_n - 1)))
        self.b.group("pe", fns, reads=[wr, ares], writes=[bres])

    def load_consts(self):
        b = self.b
        b.dma("sp", self.mds(), self.CS[:, :], self.cst, writes=[self.cs_res])
        r = Res()
        b.op("pool", lambda e: e.memset(self.ones_bf[:, :], 1.0), writes=[r])
        b.op("pool", lambda e: e.memset(self.rm_bf[:, :], 0.0), writes=[r])
        b.op("pool", lambda e: e.tensor_copy(out=self.rm_bf[:, 0:32], in_=self.cview("rm")),
             reads=[self.cs_res], writes=[r])
        b.op("pool", lambda e: e.tensor_copy(out=self.ident_bf[:, :], in_=self.cview("ident")),
             reads=[self.cs_res], writes=[r])
        mtmp = self.bview(0, [128, 2, 512], F32)
        mr = self.newB()
        b.dma("sp", self.mds(), mtmp, self.msk, writes=[mr])
        b.op("pool", lambda e: e.tensor_copy(out=self.mask_bf[:, :, :], in_=mtmp), reads=[mr], writes=[r])
        self.k_res = r

    def phase_in(self):
        b = self.b
        xt_views = [self.bview(i * 2048, [128, 2048], F32) for i in range(2)]
        xo_views = [self.bview(4096 + i * 4096, [128, KC, 256], F32) for i in range(2)]
        self.b_phase()
        xt_ring = Ring(b, "pin_l", xt_views, ndsem=self.ds_p, mk=self)
        xo_ring = Ring(b, "pin_s", xo_views, ndsem=self.ds_q, mk=self)
        ident = self.cview("ident")
        for hb in range(T // 256):
            xo, xor_, xod = xo_ring.next()
            for tt in range(2):
                t = 2 * hb + tt
                xt, xr, xd = xt_ring.next()
                b.dma("sp", xd, xt, self.x_in[t * 128:(t + 1) * 128, :], writes=[xr])
                for g in range(4):
                    bk, br = self.bank()
                    fns = [lambda e, j=j: e.transpose(out=bk[:, j * 128:(j + 1) * 128],
                                                      in_=xt[:, (4 * g + j) * 128:(4 * g + j + 1) * 128],
                                                      identity=ident) for j in range(4)]
                    b.group("pe", fns, reads=[xr, self.cs_res], writes=[br])
                    src = bk[:, :].rearrange("p (a b) -> p a b", a=4)
                    dst = xo[:, 4 * g:4 * g + 4, tt * 128:(tt + 1) * 128]
                    if g % 2 == 0:
                        b.op("dve", lambda e: e.tensor_copy(out=dst, in_=src), reads=[br], writes=[xor_])
                    else:
                        b.op("act", lambda e: e.activation(out=dst, in_=src, func=AF.Copy), reads=[br], writes=[xor_])
            b.dma("sp", xod, self.xT[hb], xo, reads=[xor_], writes=[self.xres[kc][hb] for kc in range(KC)])

    def phase_norm(self, gname, l, nb=NB):
        b = self.b
        go, _ = self.coff[gname]
        gv = self.CS[:, go + l * 16: go + (l + 1) * 16]
        self.b_phase()
        xs_ring = Ring(b, f"nrm_x{gname}{l}", [self.bview(i * 4096, [128, KC, 256], F32) for i in range(2)], ndsem=self.ds_p, mk=self)
        sq_ring = Ring(b, "nrm_q", [self.bview(8192 + i * 2048, [128, KC, 256], BF16) for i in range(2)], mk=self)
        rs_ring = Ring(b, "nrm_r", [self.bview(12288 + i * 256, [128, 256], F32) for i in range(2)], mk=self)
        for hb in range(2 * nb):
            blk = hb // 2
            t0 = hb * 256
            xs, xr, xd = xs_ring.next()
            b.dma("sp", xd, xs, self.xT[hb], reads=[self.xres[kc][hb] for kc in range(KC)], writes=[xr])
            sq, sqr = sq_ring.next()
            b.op("act", lambda e, sq=sq, xs=xs: e.activation(out=sq, in_=xs, func=AF.Square), reads=[xr], writes=[sqr])
            bk, br = self.bank(4, 6)
            fns = [lambda e, kc=kc, bk=bk, sq=sq: e.matmul(bk[:, 0:256], lhsT=self.ones_bf[:, :], rhs=sq[:, kc, :],
                                                          start=(kc == 0), stop=(kc == KC - 1)) for kc in range(KC)]
            b.group("pe", fns, reads=[sqr, self.k_res], writes=[br])
            rs, rsr = rs_ring.next()
            b.op("dve", lambda e, rs=rs, bk=bk: e.tensor_scalar(out=rs, in0=bk[:, 0:256], scalar1=1.0 / D, scalar2=None,
                                                               op0=ALU.mult), reads=[br], writes=[rsr])
            self.rsqrt_eps(rs, rsr)
            for kc in range(KC):
                b.op("dve", lambda e, kc=kc, xs=xs, rs=rs, t0=t0: e.scalar_tensor_tensor(
                    out=self.A_bf[:, kc, t0:t0 + 256], in0=xs[:, kc, :], scalar=gv[:, kc:kc + 1], in1=rs,
                    op0=ALU.mult, op1=ALU.mult), reads=[xr, rsr, self.cs_res], writes=[self.Ares[blk]])

    def load_layer_consts(self, l):
        b = self.b
        b.dma("sp", self.mds(), self.PL[:, :], self.plc[l], writes=[self.pl_res])
        b.op("pool", lambda e: e.tensor_copy(out=self.wsT_bf[:, :, :],
                                             in_=self.PL[:, 1536:2048].rearrange("p (a b) -> p a b", a=4)),
             reads=[self.pl_res], writes=[self.pl_res])
        so, _ = self.coff["sink"]
        b.op("act", lambda e: e.activation(out=self.esink[:, :], in_=self.CS[:, so + l * 8: so + l * 8 + 8], func=AF.Exp),
             reads=[self.cs_res], writes=[self.pl_res])

    def phase_proj(self, l, ntl=NT):
        nbl = (ntl + 3) // 4
        b = self.b
        A = self.A_bf
        ropev = self.bview(0, [128, 2 * T], F32)[0:32, :].rearrange("p (a b) -> p a b", a=2)
        self.b_phase()
        rope_r = self.newB()
        b.dma("sp", self.mds(), ropev, self.rope, writes=[rope_r])
        self.q32 = Ring(b, "q32", [self.bview(5120 + i * 512, [128, 512], F32) for i in range(2)], mk=self)
        self.t1 = Ring(b, "t1", [self.bview(6144 + i * 512, [128, 512], F32)[0:32, :] for i in range(2)], mk=self)
        self.t2 = Ring(b, "t2", [self.bview(7168 + i * 512, [128, 512], F32)[0:32, :] for i in range(2)], mk=self)
        qh_ring = Ring(b, "qh", [self.bview(8192 + i * 256, [128, 512], BF16) for i in range(2)], mk=self)
        ql_ring = Ring(b, "ql", [self.bview(8704 + i * 256, [128, 512], BF16) for i in range(2)], mk=self)
        rmv = self.rm_bf
        for j in range(4):
            wa, war = self.wnext(ahead=2)
            wg, wgr = self.wnext(ahead=1)
            for blk in range(nbl):
                n = self.bw(blk, ntl)
                pa, par = self.bank()
                self.gemm_fm(wa, war, KC, A, self.Ares[blk], blk, pa, par, n=n)
                pg, pgr = self.bank()
                self.gemm_fm(wg, wgr, KC, A, self.Ares[blk], blk, pg, pgr, n=n)
                tf, tfr = self.tmpf.next()
                b.op("act", lambda e, tf=tf, pg=pg: e.activation(out=tf, in_=pg[:, :], func=AF.Sigmoid),
                     reads=[pgr], writes=[tfr])
                of, ofr, ofd = self.obf.next()
                b.op("dve", lambda e, of=of, pa=pa, tf=tf: e.tensor_tensor(out=of, in0=pa[:, :], in1=tf, op=ALU.mult),
                     reads=[par, tfr], writes=[ofr])
                b.dma("sp", ofd, self.cT[j, :, blk * 512:(blk + 1) * 512], of, reads=[ofr],
                      writes=[self.dr("cT", j, blk)])
        import os
        pstop = int(os.environ.get("PROJ_STOP", "9"))
        if pstop <= 1:
            return
        def rope_tail(st):
            ob, obr, obd, q32, q32r, dst_ap, key = st
            pr, prr = self.bank(6, 8)
            ql, qlr = ql_ring.next()
            b.op("dve", lambda e: e.tensor_tensor(out=ql, in0=q32, in1=ob, op=ALU.subtract),
                 reads=[q32r, obr], writes=[qlr])
            b.group("pe", [lambda e: e.matmul(pr[:, :], lhsT=rmv[:, :], rhs=ob, start=True, stop=False),
                           lambda e: e.matmul(pr[:, :], lhsT=rmv[:, :], rhs=ql, start=False, stop=True)],
                    reads=[obr, qlr, self.k_res], writes=[prr])
            blk_ = key[1]
            t1, t1r = self.t1.next()
            b.op("dve", lambda e: e.tensor_tensor(out=t1, in0=q32[0:32, :], in1=ropev[:, 0, blk_ * 512:(blk_ + 1) * 512],
                                                  op=ALU.mult), reads=[q32r, rope_r], writes=[t1r])
            t2, t2r = self.t2.next()
            b.op("dve", lambda e: e.tensor_tensor(out=t2, in0=pr[0:32, :], in1=ropev[:, 1, blk_ * 512:(blk_ + 1) * 512],
                                                  op=ALU.mult), reads=[prr, rope_r], writes=[t2r])
            b.op("dve", lambda e: e.tensor_tensor(out=ob[0:32, :], in0=t1, in1=t2, op=ALU.add),
                 reads=[t1r, t2r], writes=[obr])
            b.dma("sp", obd, dst_ap, ob, reads=[obr], writes=[self.dr("qk", *key)])

        pending = None
        for h in range(10):
            wt, wr = self.wnext()
            dst = self.qT[h] if h < 8 else self.kT[h - 8]
            for blk in range(nbl):
                bk, br = self.bank()
                self.gemm_fm(wt, wr, KC, A, self.Ares[blk], blk, bk, br, n=self.bw(blk, ntl))
                ob, obr, obd = self.obf.next()
                q32, q32r = self.q32.next()
                b.op("dve", lambda e: e.tensor_copy(out=q32, in_=bk[:, :]), reads=[br], writes=[q32r])
                b.op("act", lambda e: e.activation(out=ob, in_=q32, func=AF.Copy), reads=[q32r], writes=[obr])
                if pending is not None:
                    rope_tail(pending)
                pending = (ob, obr, obd, q32, q32r, dst[:, blk * 512:(blk + 1) * 512], (h, blk))
        rope_tail(pending)
        if pstop <= 2:
            return
        wv0, wv0r = self.wnext(ahead=2)
        wv1, wv1r = self.wnext(ahead=1)
        for i2 in range((ntl + 1) // 2):
            bk, br = self.bank()
            fns = []
            for jj in range(2):
                t = 2 * i2 + jj
                if t >= ntl:
                    continue
                for c, wt in enumerate((wv0, wv1)):
                    col = jj * 256 + c * 128
                    for kc in range(KC):
                        fns.append(lambda e, t=t, kc=kc, wt=wt, col=col: e.matmul(
                            bk[:, col:col + 128], lhsT=A[:, kc, t * 128:(t + 1) * 128], rhs=wt[:, kc, :],
                            start=(kc == 0), stop=(kc == KC - 1)))
            b.group("pe", fns, reads=[wv0r, wv1r, self.Ares[i2 // 2]], writes=[br])
            ob, obr, obd = self.obf.next()
            b.op("act", lambda e: e.activation(out=ob, in_=bk[:, :], func=AF.Copy), reads=[br], writes=[obr])
            b.dma("sp", obd, self.V[:, 2 * i2:2 * i2 + 2, :], ob.rearrange("p (a b) -> p a b", a=2),
                  reads=[obr], writes=[self.dr("V", i2)])
        if pstop <= 3:
            return
        for j in range(4):
            wt, wr = self.wnext()
            for blk in range(nbl):
                bk, br = self.bank()
                self.gemm_fm(wt, wr, KC, A, self.Ares[blk], blk, bk, br, n=self.bw(blk, ntl))
                ob, obr, obd = self.obf.next()
                b.op("act", lambda e, ob=ob, bk=bk: e.activation(out=ob, in_=bk[:, :], func=AF.Gelu),
                     reads=[br], writes=[obr])
                b.dma("sp", obd, self.uT[j, :, blk * 512:(blk + 1) * 512], ob, reads=[obr],
                      writes=[self.dr("uT", j, blk)])
        if pstop <= 4:
            return
        for c in range(4):
            wt, wr = self.wnext()
            for i in range(nbl):
                bk, br = self.bank()
                fns = []
                for jj in range(4):
                    t = 4 * i + jj
                    if t >= ntl:
                        continue
                    for kc in range(KC):
                        fns.append(lambda e, jj=jj, t=t, kc=kc, bk=bk, wt=wt: e.matmul(
                            bk[:, jj * 128:(jj + 1) * 128], lhsT=A[:, kc, t * 128:(t + 1) * 128], rhs=wt[:, kc, :],
                            start=(kc == 0), stop=(kc == KC - 1)))
                b.group("pe", fns, reads=[wr, self.Ares[i]], writes=[br])
                of, ofr, ofd = self.of32.next()
                b.op("act", lambda e, of=of, bk=bk: e.activation(out=of, in_=bk[:, :], func=AF.Gelu),
                     reads=[br], writes=[ofr])
                b.dma("sp", ofd, self.vg[:, 4 * i:4 * i + 4, c * 128:(c + 1) * 128],
                      of.rearrange("p (a b) -> p a b", a=4), reads=[ofr], writes=[self.dr("vg", c, i)])

    def phase_attn(self, l, nb=NB, ntm=NT):
        b = self.b
        A = self.A_bf
        scale = 1.0 / float(np.sqrt(128.0))
        qv = self.bview(0, [128, 4, T], BF16)
        kv = self.bview(5120, [128, T], BF16)
        vv = self.bview(6400, [128, NT, 256], BF16)
        ev = [self.bview(8960 + i * 256, [128, 512], BF16) for i in range(6)]
        esb = self.bview(10496, [128, 1024], F32)
        self.b_phase()
        e_ring = Ring(b, "att_e", ev, mk=self)
        qr, kr, vr = self.newB(), self.newB(), self.newB()
        esr = self.newB()
        for h in range(8):
            b.op("pool", lambda e: e.tensor_copy(out=esb[:, h * 128:(h + 1) * 128],
                                                 in_=self.esink[:, h:h + 1].to_broadcast([128, 128])),
                 reads=[self.pl_res], writes=[esr])
        b.dma("sp", self.mds(), vv, self.V, reads=[self.dr("V", i2) for i2 in range(NT // 2)], writes=[vr])
        for g in range(2):
            b.dma("sp", self.mds(), qv, self.qT[4 * g:4 * g + 4].rearrange("h p t -> p h t"),
                  reads=[self.dr("qk", h, blk) for h in range(4 * g, 4 * g + 4) for blk in range(NB)], writes=[qr])
            b.dma("sp", self.mds(), kv, self.kT[g],
                  reads=[self.dr("qk", 8 + g, blk) for blk in range(NB)], writes=[kr])
            for t in range(min(4 * nb, ntm)):
                kts = [kt for kt in (t - 1, t, t + 1) if 0 <= kt < NT]
                qrhs = qv[:, :, t * 128:(t + 1) * 128]
                es = []
                for kt in kts:
                    sb_, sbr = self.bank(0, 6)
                    fns = [lambda e, sb_=sb_, kt=kt, qrhs=qrhs: e.matmul(
                        sb_[:, :], lhsT=kv[:, kt * 128:(kt + 1) * 128], rhs=qrhs, start=True, stop=(kt == t))]
                    if kt != t:
                        mi = 0 if kt < t else 1
                        fns.append(lambda e, sb_=sb_, mi=mi: e.matmul(
                            sb_[:, :], lhsT=self.ident_bf[:, :], rhs=self.mask_bf[:, mi, :], start=False, stop=True))
                    b.group("pe", fns, reads=[qr, kr, self.k_res], writes=[sbr])
                    ee, eer = e_ring.next()
                    b.op("act", lambda e, ee=ee, sb_=sb_: e.activation(out=ee, in_=sb_[:, :], func=AF.Exp, scale=scale),
                         reads=[sbr], writes=[eer])
                    es.append((kt, ee, eer))
                dn, dnr = self.banks[6], self.bres[6]
                ob_, obr_ = self.banks[7], self.bres[7]
                n = len(es)
                b.group("pe", [lambda e, i=i, ee=ee: e.matmul(dn[:, :], lhsT=self.ones_bf[:, :], rhs=ee,
                                                              start=(i == 0), stop=(i == n - 1))
                               for i, (kt, ee, eer) in enumerate(es)],
                        reads=[x[2] for x in es] + [self.k_res], writes=[dnr])
                b.group("pe", [lambda e, i=i, ee=ee, kt=kt: e.matmul(ob_[:, :], lhsT=vv[:, kt, g * 128:(g + 1) * 128], rhs=ee,
                                                                     start=(i == 0), stop=(i == n - 1))
                               for i, (kt, ee, eer) in enumerate(es)],
                        reads=[x[2] for x in es] + [vr], writes=[obr_])
                ds_, dsr = self.small.next()
                b.op("dve", lambda e: e.tensor_tensor(out=ds_, in0=dn[:, :], in1=esb[:, g * 512:(g + 1) * 512], op=ALU.add),
                     reads=[dnr, esr], writes=[dsr])
                b.op("act", lambda e: e.activation(out=ds_, in_=ds_, func=AF.Ln), reads=[dsr], writes=[dsr])
                b.op("act", lambda e: e.activation(out=ds_, in_=ds_, func=AF.Exp, scale=-1.0), reads=[dsr], writes=[dsr])
                b.op("dve", lambda e, ds_=ds_, t=t, g=g: e.tensor_tensor(
                    out=A[:, 4 * g:4 * g + 4, t * 128:(t + 1) * 128],
                    in0=ob_[:, :].rearrange("p (a b) -> p a b", a=4),
                    in1=ds_.rearrange("p (a b) -> p a b", a=4), op=ALU.mult),
                    reads=[obr_, dsr], writes=[self.Ares[t // 4]])

    def phase_conv(self, l, nb=NB, ntm=NT):
        b = self.b
        A = self.A_bf
        dg = self.bview(0, [128, 124, 128], BF16)
        cbs = [self.bview(7936 + i * 1088, [128, 4, 544], BF16) for i in range(2)]
        yb = [[self.bview(10112 + j * 512, [128, 512], F32) for j in range(4)]]
        ysq = [self.bview(12160 + i * 256, [128, 512], BF16) for i in range(2)]
        self.b_phase()
        dgr = self.newB()
        cbr = [self.newB() for _ in range(2)]
        yres = [[self.newB() for _ in range(4)]]
        ysq_ring = Ring(b, "cv_sq", ysq, mk=self)
        yh_ring = Ring(b, "cv_yh", [self.bview(12672, [128, 512], BF16)], mk=self)
        yl_ring = Ring(b, "cv_yl", [self.bview(12928, [128, 512], BF16)], mk=self)
        cds = self.cv_ds
        wo, _ = self.coff["dww"]
        bo, _ = self.coff["dwb"]
        for j in range(4):
            for k in range(31):
                w0 = wo + (l * 4 + j) * 31 + k
                b.op("dve", lambda e: e.tensor_scalar(out=dg[:, j * 31 + k, :], in0=self.ident_bf[:, :],
                                                      scalar1=self.CS[:, w0:w0 + 1], scalar2=None, op0=ALU.mult),
                     reads=[self.cs_res, self.k_res], writes=[dgr])
        for blk in range(nb):
            par = 0
            for sh in range(2):
                lo = blk * 512 - 15 + sh
                hi = lo + 542
                cb, cr = cbs[sh], cbr[sh]
                slo, shi = max(lo, 0), min(hi, T)
                if lo < 0 or hi > T:
                    b.op("pool", lambda e: e.memset(cb[:, :, :], 0.0), writes=[cr])
                rd = [self.dr("cT", j, bb) for j in range(4) for bb in range(max(blk - 1, 0), min(blk + 2, NB))]
                b.dma("sp", cds[(2 * blk + sh) % 4], cb[:, :, slo - lo:shi - lo],
                      self.cT[:, :, slo:shi].rearrange("j p t -> p j t"), reads=rd, writes=[cr])
            for j in range(4):
                y_, yr = yb[par][j], yres[par][j]
                bk, br = self.bank()
                n = self.bw(blk, ntm)
                b.group("pe", [lambda e, k=k: e.matmul(bk[:, 0:n], lhsT=dg[:, j * 31 + k, :],
                                                       rhs=cbs[k % 2][:, j, k - (k % 2):k - (k % 2) + n],
                                                       start=(k == 0), stop=(k == 30)) for k in range(31)],
                        reads=[dgr] + cbr, writes=[br])
                b.op("dve", lambda e: e.tensor_scalar(out=y_, in0=bk[:, :],
                                                      scalar1=self.CS[:, bo + l * 4 + j: bo + l * 4 + j + 1],
                                                      scalar2=None, op0=ALU.add),
                     reads=[br, self.cs_res], writes=[yr])
            sm, smr = self.banks[4], self.bres[4]
            s2, s2r = self.banks[5], self.bres[5]
            for j in range(4):
                yh, yhr = yh_ring.next()
                yl, ylr = yl_ring.next()
                b.op("dve", lambda e: e.tensor_copy(out=yh, in_=yb[par][j]), reads=[yres[par][j]], writes=[yhr])
                b.op("dve", lambda e: e.tensor_tensor(out=yl, in0=yb[par][j], in1=yh, op=ALU.subtract),
                     reads=[yres[par][j], yhr], writes=[ylr])
                b.group("pe", [lambda e: e.matmul(sm[:, :], lhsT=self.ones_bf[:, :], rhs=yh, start=(j == 0), stop=False),
                               lambda e: e.matmul(sm[:, :], lhsT=self.ones_bf[:, :], rhs=yl, start=False, stop=(j == 3))],
                        reads=[yhr, ylr, self.k_res], writes=[smr])
                sq, sqr = ysq_ring.next()
                b.op("act", lambda e: e.activation(out=sq, in_=yb[par][j], func=AF.Square),
                     reads=[yres[par][j]], writes=[sqr])
                b.group("pe", [lambda e: e.matmul(s2[:, :], lhsT=self.ones_bf[:, :], rhs=sq,
                                                  start=(j == 0), stop=(j == 3))],
                        reads=[sqr, self.k_res], writes=[s2r])
            mean, mr = self.small.next()
            rstd, rr = self.small.next()
            b.op("dve", lambda e, mean=mean: e.tensor_scalar(out=mean, in0=sm[:, :], scalar1=1.0 / 512, scalar2=None,
                                                             op0=ALU.mult), reads=[smr], writes=[mr])
            tf, tfr = self.tmpf.next()
            b.op("dve", lambda e, tf=tf, mean=mean: e.tensor_tensor(out=tf, in0=mean, in1=mean, op=ALU.mult),
                 reads=[mr], writes=[tfr])
            b.op("dve", lambda e, rstd=rstd, tf=tf: e.scalar_tensor_tensor(
                out=rstd, in0=s2[:, :], scalar=1.0 / 512, in1=tf, op0=ALU.mult, op1=ALU.subtract),
                reads=[s2r, tfr], writes=[rr])
            self.rsqrt_eps(rstd, rr)
            go, _ = self.coff["clg"]
            bo2, _ = self.coff["clb"]
            for j in range(4):
                y_, yr = yb[par][j], yres[par][j]
                tf, tfr = self.tmpf.next()
                b.op("dve", lambda e, tf=tf, y_=y_, mean=mean: e.tensor_tensor(out=tf, in0=y_, in1=mean, op=ALU.subtract),
                     reads=[yr, mr], writes=[tfr])
                b.op("dve", lambda e, tf=tf, rstd=rstd: e.tensor_tensor(out=tf, in0=tf, in1=rstd, op=ALU.mult),
                     reads=[tfr, rr], writes=[tfr])
                b.op("dve", lambda e, tf=tf, j=j: e.tensor_scalar(
                    out=tf, in0=tf, scalar1=self.CS[:, go + l * 4 + j: go + l * 4 + j + 1],
                    scalar2=self.CS[:, bo2 + l * 4 + j: bo2 + l * 4 + j + 1], op0=ALU.mult, op1=ALU.add),
                    reads=[tfr, self.cs_res], writes=[tfr])
                b.op("act", lambda e, tf=tf, j=j, blk=blk: e.activation(
                    out=A[:, 8 + j, blk * 512:(blk + 1) * 512], in_=tf, func=AF.Silu),
                    reads=[tfr], writes=[self.Ares[blk]])

    def phase_sgu(self, l, nb=NB, ntm=NT):
        b = self.b
        A = self.A_bf
        ub = self.bview(0, [128, 4, T], BF16)
        vgb = [self.bview(5120 + i * 2048, [128, 4, 512], F32) for i in range(2)]
        vnb = [self.bview(9216 + i * 1024, [128, 4, 512], BF16) for i in range(2)]
        st6 = self.bview(11264, [128, 8], F32)
        mv = self.bview(11272, [128, 8], F32)
        self.b_phase()
        vg_ring = Ring(b, f"sg_v{l}", vgb, ndsem=self.ds_p, mk=self)
        vn_ring = Ring(b, "sg_n", vnb, mk=self)
        ur = self.newB()
        sres = self.newB()
        b.dma("sp", self.mds(), ub, self.uT.rearrange("j p t -> p j t"),
              reads=[self.dr("uT", j, blk) for j in range(4) for blk in range(NB)], writes=[ur])
        lng = self.PL[:, 0:512]
        lnb = self.PL[:, 512:1024]
        bsb = self.PL[:, 1024:1536].rearrange("p (a b) -> p a b", a=4)
        for i in range(nb):
            vgt, vgr, vgd = vg_ring.next()
            b.dma("sp", vgd, vgt, self.vg[:, 4 * i:4 * i + 4, :],
                  reads=[self.dr("vg", c, i) for c in range(4)], writes=[vgr])
            vn, vnr = vn_ring.next()
            for jj in range(4):
                b.op("dve", lambda e, vgt=vgt, jj=jj: e.bn_stats(out=st6[:, 0:6], in_=vgt[:, jj, :]),
                     reads=[vgr], writes=[sres])
                b.op("dve", lambda e: e.bn_aggr(out=mv[:, 0:2], in_=st6[:, 0:6]), reads=[sres], writes=[sres])
                self.rsqrt_eps(mv[:, 1:2], sres)
                b.op("dve", lambda e, vgt=vgt, jj=jj: e.tensor_scalar(
                    out=vgt[:, jj, :], in0=vgt[:, jj, :], scalar1=mv[:, 0:1], scalar2=mv[:, 1:2],
                    op0=ALU.subtract, op1=ALU.mult), reads=[vgr, sres], writes=[vgr])
                b.op("dve", lambda e, vgt=vgt, jj=jj: e.tensor_tensor(out=vgt[:, jj, :], in0=vgt[:, jj, :], in1=lng, op=ALU.mult),
                     reads=[vgr, self.pl_res], writes=[vgr])
                b.op("dve", lambda e, vgt=vgt, vn=vn, jj=jj: e.tensor_tensor(out=vn[:, jj, :], in0=vgt[:, jj, :], in1=lnb, op=ALU.add),
                     reads=[vgr, self.pl_res], writes=[vnr])
            for h in range(4):
                bk, br = self.bank(0, 4)
                b.group("pe", [lambda e, jj=jj, h=h, bk=bk, vn=vn: e.matmul(
                    bk[:, jj * 128:(jj + 1) * 128], lhsT=vn[:, jj, h * 128:(h + 1) * 128], rhs=self.wsT_bf[:, h, :],
                    start=True, stop=True) for jj in range(4) if 4 * i + jj < ntm], reads=[vnr, self.pl_res], writes=[br])
                tf, tfr = self.tmpf.next()
                b.op("dve", lambda e, tf=tf, bk=bk, h=h: e.tensor_tensor(
                    out=tf.rearrange("p (a b) -> p a b", a=4), in0=bk[:, :].rearrange("p (a b) -> p a b", a=4),
                    in1=bsb[:, h:h + 1, :].to_broadcast([128, 4, 128]), op=ALU.add),
                    reads=[br, self.pl_res], writes=[tfr])
                b.op("dve", lambda e, tf=tf, h=h, i=i: e.tensor_tensor(
                    out=A[:, 12 + h, i * 512:(i + 1) * 512], in0=tf, in1=ub[:, h, i * 512:(i + 1) * 512], op=ALU.mult),
                    reads=[tfr, ur], writes=[self.Ares[i]])

    def residual_prefetch(self, j, blk):
        xi, xir, xid = self.xin.next()
        sl = self.xT[2 * blk:2 * blk + 2, :, j, :].rearrange("h p t -> p h t")
        xrs = [self.xres[j][2 * blk], self.xres[j][2 * blk + 1]]
        self.b.dma("sp", xid, xi.rearrange("p (h t) -> p h t", h=2), sl, reads=xrs, writes=[xir])
        return xi, xir, sl, xrs

    def residual_epilogue(self, bk, br, pre):
        b = self.b
        xi, xir, sl, xrs = pre
        xo, xor_, xod = self.xout.next()
        b.op("dve", lambda e, xo=xo, xi=xi, bk=bk: e.tensor_tensor(out=xo, in0=bk[:, :], in1=xi, op=ALU.add),
             reads=[br, xir], writes=[xor_])
        b.dma("sp", xod, sl, xo.rearrange("p (h t) -> p h t", h=2), reads=[xor_], writes=xrs)

    def residual_gemm_loop(self, kc_n, act, actres, nb=NB, ntm=NT):
        LA = 3
        items = [(j, blk) for j in range(KC) for blk in range(nb)]
        pre = {}
        for i in range(min(LA, len(items))):
            pre[i] = self.residual_prefetch(*items[i])
        wt = wr = None
        for i, (j, blk) in enumerate(items):
            if blk == 0:
                wt, wr = self.wnext()
            if i + LA < len(items):
                pre[i + LA] = self.residual_prefetch(*items[i + LA])
            bk, br = self.bank()
            self.gemm_fm(wt, wr, kc_n, act, actres[blk], blk, bk, br, n=self.bw(blk, ntm))
            self.residual_epilogue(bk, br, pre.pop(i))

    def phase_wout(self, l, nb=NB, ntm=NT):
        self.residual_gemm_loop(KC, self.A_bf, self.Ares, nb, ntm)

    def phase_ffn(self, l, nb=NB, ntm=NT):
        b = self.b
        act = self.bview(0, [128, FPP, T], BF16)
        self.b_phase()
        actres = [self.newB() for _ in range(NB)]
        for p in range(NPASS):
            for fi in range(FPP):
                wg, wgr = self.wnext(ahead=2)
                wu, wur = self.wnext(ahead=1)
                for blk in range(nb):
                    n = self.bw(blk, ntm)
                    pg, pgr = self.bank()
                    self.gemm_fm(wg, wgr, KC, self.A_bf, self.Ares[blk], blk, pg, pgr, n=n)
                    pu, pur = self.bank()
                    self.gemm_fm(wu, wur, KC, self.A_bf, self.Ares[blk], blk, pu, pur, n=n)
                    tf, tfr = self.tmpf.next()
                    b.op("act", lambda e, tf=tf, pg=pg: e.activation(out=tf, in_=pg[:, :], func=AF.Silu),
                         reads=[pgr], writes=[tfr])
                    b.op("dve", lambda e, tf=tf, pu=pu, fi=fi, blk=blk: e.tensor_tensor(
                        out=act[:, fi, blk * 512:(blk + 1) * 512], in0=pu[:, :], in1=tf, op=ALU.mult),
                        reads=[pur, tfr], writes=[actres[blk]])
            self.residual_gemm_loop(FPP, act, actres, nb, ntm)

    def phase_final(self):
        b = self.b
        go, _ = self.coff["gf"]
        gv = self.CS[:, go:go + 16]
        ident = self.cview("ident")
        self.b_phase()
        xs_ring = Ring(b, "fin_x", [self.bview(i * 4096, [128, KC, 256], F32) for i in range(2)], ndsem=self.ds_p, mk=self)
        sq_ring = Ring(b, "fin_q", [self.bview(8192 + i * 2048, [128, KC, 256], BF16) for i in range(1)], mk=self)
        rs_ring = Ring(b, "fin_r", [self.bview(10240 + i * 256, [128, 256], F32) for i in range(1)], mk=self)
        ot_views = [self.A[:, i * 2048:(i + 1) * 2048] for i in range(2)]
        ot_ring = Ring(b, "fin_o", ot_views, ndsem=self.ds_q)
        a_all = list(self.Ares)
        self.final_evs = []
        for hb in range(OWN // 256):
            blk = hb // 2
            t0 = hb * 256
            xs, xr, xd = xs_ring.next()
            b.dma("sp", xd, xs, self.xT[hb], reads=[self.xres[kc][hb] for kc in range(KC)], writes=[xr])
            sq, sqr = sq_ring.next()
            b.op("act", lambda e, sq=sq, xs=xs: e.activation(out=sq, in_=xs, func=AF.Square), reads=[xr], writes=[sqr])
            bk, br = self.bank(4, 6)
            fns = [lambda e, kc=kc, bk=bk, sq=sq: e.matmul(bk[:, 0:256], lhsT=self.ones_bf[:, :], rhs=sq[:, kc, :],
                                                          start=(kc == 0), stop=(kc == KC - 1)) for kc in range(KC)]
            b.group("pe", fns, reads=[sqr, self.k_res], writes=[br])
            rs, rsr = rs_ring.next()
            b.op("dve", lambda e, rs=rs, bk=bk: e.tensor_scalar(out=rs, in0=bk[:, 0:256], scalar1=1.0 / D, scalar2=None,
                                                               op0=ALU.mult), reads=[br], writes=[rsr])
            self.rsqrt_eps(rs, rsr)
            for kc in range(KC):
                b.op("dve", lambda e, kc=kc, xs=xs, rs=rs: e.scalar_tensor_tensor(
                    out=xs[:, kc, :], in0=xs[:, kc, :], scalar=gv[:, kc:kc + 1], in1=rs,
                    op0=ALU.mult, op1=ALU.mult), reads=[xr, rsr, self.cs_res], writes=[xr])
            for tt in range(2):
                ot, otr, otd = ot_ring.next()
                for g in range(4):
                    pb, pbr = self.bank(0, 4)
                    fns = [lambda e, j=j, g=g, pb=pb, xs=xs, tt=tt: e.transpose(
                        out=pb[:, j * 128:(j + 1) * 128], in_=xs[:, 4 * g + j, tt * 128:(tt + 1) * 128],
                        identity=ident) for j in range(4)]
                    b.group("pe", fns, reads=[xr, self.cs_res], writes=[pbr])
                    if g % 2 == 0:
                        b.op("dve", lambda e, ot=ot, pb=pb, g=g: e.tensor_copy(out=ot[:, g * 512:(g + 1) * 512], in_=pb[:, :]),
                             reads=[pbr], writes=[otr] + a_all)
                    else:
                        b.op("act", lambda e, ot=ot, pb=pb, g=g: e.activation(out=ot[:, g * 512:(g + 1) * 512], in_=pb[:, :], func=AF.Copy),
                             reads=[pbr], writes=[otr] + a_all)
                    a_all = []
                tok0 = t0 + tt * 128
                ev = b.dma("sp", otd, self.out[tok0:tok0 + 128, :], ot, reads=[otr], writes=[Res()])
                self.final_evs.append(ev)

    def finish(self):
        b = self.b
        for d in b.all_ds:
            if d.count:
                b._wait("sp", (d.key, d.count))
        b.streams["sp"].append(("ins", lambda e: e.nop(), None, 0))

    def weight_tiles(self):
        tiles = []
        for l in range(self.L):
            for i in range(NWIN):
                tiles.append((self.w_in[l, i], KC))
            for i in range(KC):
                tiles.append((self.w_out[l, i], KC))
            for p in range(NPASS):
                for fi in range(FPP):
                    f = p * FPP + fi
                    tiles.append((self.w_gate[l, f], KC))
                    tiles.append((self.w_up[l, f], KC))
                for j in range(KC):
                    tiles.append((self.w_down[l, p, j], FPP))
        return tiles

    def build(self, upto=None):
        order = ["in", "norm1", "proj", "attn", "conv", "sgu", "wout", "norm2", "ffn"]
        stop = len(order) if upto is None else order.index(upto) + 1
        with contextlib.ExitStack() as st:
            self.setup(st)
            self.load_consts()
            self.wstream_init(self.weight_tiles())
            if stop >= 1:
                self.phase_in()
            for l in range(self.L):
                self.load_layer_consts(l)
                nb = self.nbm(l)
                ntm = self.ntm(l)
                ntl = min(NT, ntm + 1)
                steps = [None, lambda: self.phase_norm("g1", l), lambda: self.phase_proj(l, ntl),
                         lambda: self.phase_attn(l, nb, ntm), lambda: self.phase_conv(l, nb, ntm),
                         lambda: self.phase_sgu(l, nb, ntm), lambda: self.phase_wout(l, nb, ntm),
                         lambda: self.phase_norm("g2", l, nb), lambda: self.phase_ffn(l, nb, ntm)]
                for i in range(1, min(stop, len(order))):
                    if self.debug and l == 0 and order[i] == "wout":
                        self.b.dma("sp", self.mds(), self.dbgA, self.A_bf, reads=self.Ares, writes=[Res()])
                    steps[i]()
            if upto is None:
                self.phase_final()
            self.finish()
            self.b.replay()
        return self.nc


def _win_perm():
    cols = []
    for j in range(4):
        cols += list(range(1536 + j * 128, 1536 + (j + 1) * 128))
        cols += list(range(2048 + j * 128, 2048 + (j + 1) * 128))
    cols += list(range(0, 1024))
    cols += list(range(1024, 1280))
    cols += list(range(1280, 1536))
    cols += list(range(2560, 3072))
    cols += list(range(3072, 3584))
    return np.array(cols)


def _tile_kc(w):
    L, K, N = w.shape
    return np.ascontiguousarray(w.reshape(L, K // 128, 128, N // 128, 128).transpose(0, 3, 2, 1, 4))


def prep_weights(w_in, w_out, w_gate, w_up, w_down):
    L = w_in.shape[0]
    d = {}
    d["w_in"] = _tile_kc(np.asarray(w_in)[:, :, _win_perm()])
    d["w_out"] = _tile_kc(np.asarray(w_out))
    d["w_gate"] = _tile_kc(np.asarray(w_gate))
    d["w_up"] = _tile_kc(np.asarray(w_up))
    wd = np.asarray(w_down).reshape(L, NPASS, FPP, 128, KC, 128)
    d["w_down"] = np.ascontiguousarray(wd.transpose(0, 1, 4, 3, 2, 5))
    return d


def prep_consts(L, mix_norm_g, ffn_norm_g, final_norm_g, conv_dw_w, conv_dw_b, conv_ln_g, conv_ln_b,
                sink, sgu_ln_g, sgu_ln_b, sgu_w, sgu_b, mirror=False):
    if mirror:
        conv_dw_w = np.asarray(conv_dw_w)[:, ::-1, :]
        sgu_w = np.asarray(sgu_w)[:, :, ::-1, ::-1]
        sgu_b = np.asarray(sgu_b)[:, :, ::-1]
    off, words = _const_layout(L)
    cst = np.zeros((128, words), np.float32)

    def put(name, arr):
        o, n = off[name]
        assert arr.shape == (128, n), (name, arr.shape, n)
        cst[:, o:o + n] = arr

    f = np.float32
    put("g1", np.asarray(mix_norm_g, f).reshape(L, 16, 128).transpose(2, 0, 1).reshape(128, L * 16))
    put("g2", np.asarray(ffn_norm_g, f).reshape(L, 16, 128).transpose(2, 0, 1).reshape(128, L * 16))
    put("gf", np.asarray(final_norm_g, f).reshape(16, 128).T)
    put("dww", np.asarray(conv_dw_w, f).reshape(L, 31, 4, 128).transpose(3, 0, 2, 1).reshape(128, L * 4 * 31))
    put("dwb", np.asarray(conv_dw_b, f).reshape(L, 4, 128).transpose(2, 0, 1).reshape(128, L * 4))
    put("clg", np.asarray(conv_ln_g, f).reshape(L, 4, 128).transpose(2, 0, 1).reshape(128, L * 4))
    put("clb", np.asarray(conv_ln_b, f).reshape(L, 4, 128).transpose(2, 0, 1).reshape(128, L * 4))
    put("sink", np.broadcast_to(np.asarray(sink, f).reshape(1, L * 8), (128, L * 8)))
    put("ident", np.eye(128, dtype=f))
    rm = np.zeros((128, 32), f)
    for m in range(32):
        rm[(m + 16) % 32, m] = 1.0
    put("rm", rm)
    kk = np.arange(128)[:, None]
    qq = np.arange(128)[None, :]
    mL = np.where(kk >= qq, 0.0, NEG).astype(f)
    mR = np.where(kk <= qq, 0.0, NEG).astype(f)
    msk = np.stack([np.tile(mL, (1, 4)), np.tile(mR, (1, 4))], axis=1).astype(f)
    plc = np.zeros((L, 128, PL_WORDS), f)
    plc[:, :, 0:512] = np.asarray(sgu_ln_g, f)[:, None, :]
    plc[:, :, 512:1024] = np.asarray(sgu_ln_b, f)[:, None, :]
    plc[:, :, 1024:1536] = np.asarray(sgu_b, f).reshape(L, 1, 512)
    plc[:, :, 1536:2048] = np.asarray(sgu_w, f).transpose(0, 3, 1, 2).reshape(L, 128, 512)
    return cst, plc, msk


def rope_table(s0, mirror=False):
    pos = np.arange(s0, s0 + T, dtype=np.float32)
    if mirror:
        pos = pos[::-1]
    inv = (np.float32(500000.0) ** (-np.arange(0, 32, 2, dtype=np.float32) / np.float32(32))).astype(np.float32)
    ang = (pos[:, None] * inv[None, :]).astype(np.float32)
    c = np.cos(ang).astype(np.float32).T
    s = np.sin(ang).astype(np.float32).T
    tab = np.zeros((32, 2, T), np.float32)
    tab[0:16, 0] = c
    tab[16:32, 0] = c
    tab[0:16, 1] = -s
    tab[16:32, 1] = s
    return tab


_PROG = {}


def get_prog(L, debug=False):
    key = (L, debug)
    if key not in _PROG:
        _PROG[key] = MK(L, debug=debug).build()
    return _PROG[key]


def core_starts():
    return [(c // 2, 0 if c % 2 == 0 else SEQ - T) for c in range(8)]


def kernel(x, mix_norm_g, w_in, sink, conv_dw_w, conv_dw_b, conv_ln_g, conv_ln_b,
           sgu_ln_g, sgu_ln_b, sgu_w, sgu_b, w_out, ffn_norm_g, w_gate, w_up, w_down,
           final_norm_g):
    x = np.asarray(x, np.float32)
    L = DEPTH
    wd = prep_weights(np.asarray(w_in, np.float32), np.asarray(w_out, np.float32), np.asarray(w_gate, np.float32),
                      np.asarray(w_up, np.float32), np.asarray(w_down, np.float32))
    consts = [prep_consts(L, mix_norm_g, ffn_norm_g, final_norm_g, conv_dw_w, conv_dw_b, conv_ln_g, conv_ln_b,
                          sink, sgu_ln_g, sgu_ln_b, sgu_w, sgu_b, mirror=mir) for mir in (False, True)]
    nc = get_prog(L)
    in_maps = []
    for (bi, s0) in core_starts():
        mir = s0 != 0
        cst, plc, msk = consts[int(mir)]
        m = dict(wd)
        xs = x[bi, s0:s0 + T, :]
        m["x_in"] = np.ascontiguousarray(xs[::-1] if mir else xs)
        m["cst"] = cst
        m["plc"] = plc
        m["msk"] = msk
        m["rope"] = rope_table(s0, mirror=mir)
        in_maps.append(m)
    res = run_bass_kernel_spmd(nc, in_maps, core_ids=list(range(8)))
    out = np.empty((BATCH, SEQ, D), np.float32)
    for c, (bi, s0) in enumerate(core_starts()):
        o = res.results[c]["out"]
        if s0 == 0:
            out[bi, 0:OWN] = o
        else:
            out[bi, SEQ - OWN:SEQ] = o[::-1]
    return out
```
